# Optimizing a Trainium2 kernel written in Bass

```python
import jax, jax.numpy as jnp
from jax import lax
import numpy as np

D_MODEL = 4096
BATCH = 2
SEQ = 8192
DEPTH = 2

CTX_LEN = 256
GRID_W = 64
ML_HEADS = 6
ML_DQK = 128
ML_DV = 256
ML_CHUNK = 64
NA_HEADS = 8
NA_DH = 128
NA_WIN_ROWS = 8
NA_WIN_COLS = 16
RW_HEADS = 24
RW_N = 64
RW_DECAY_RANK = 128
RW_A_RANK = 128
RW_GATE_RANK = 480
RW_GN_EPS = 64e-5
MLP_HIDDEN = 4 * D_MODEL
ROPE_BASE = 10000.0
NORM_EPS = 1e-6

ML_QK = ML_HEADS * ML_DQK
ML_V = ML_HEADS * ML_DV
NA_W = NA_HEADS * NA_DH
RW_W = RW_HEADS * RW_N
MIX_W = ML_V + NA_W + RW_W
ML_COLS = 2 * ML_QK + 2 * ML_V + 4 * ML_HEADS
NA_IN = 3 * NA_W
RW_COLS = 3 * RW_W + RW_DECAY_RANK + RW_A_RANK + RW_GATE_RANK
IN_COLS = ML_COLS + NA_IN + RW_COLS
ML_SPLITS = (ML_QK, 2 * ML_QK, 2 * ML_QK + ML_V, 2 * ML_QK + 2 * ML_V)
RW_SPLITS = (RW_W, 2 * RW_W, 3 * RW_W, 3 * RW_W + RW_DECAY_RANK, 3 * RW_W + RW_DECAY_RANK + RW_A_RANK)

kernel_name = 'hybrid_mlstm_natten_rwkv7_dit_trunk'


def rms_norm(x, g):
    xf = x.astype(jnp.float32)
    y = xf * lax.rsqrt(jnp.mean(xf * xf, axis=-1, keepdims=True) + NORM_EPS)
    return y.astype(x.dtype) * g


def modulate(x, shift, scale):
    return x * (1 + scale) + shift


def _rope_axis(x, pos):
    half = x.shape[-1] // 2
    inv = ROPE_BASE ** (-jnp.arange(half, dtype=jnp.float32) / half)
    ang = pos.astype(jnp.float32)[:, None] * inv
    cos = jnp.cos(ang)[:, None, :]
    sin = jnp.sin(ang)[:, None, :]
    xf = x.astype(jnp.float32)
    x1, x2 = xf[..., :half], xf[..., half:]
    return jnp.concatenate([x1 * cos - x2 * sin, x2 * cos + x1 * sin], axis=-1)


def rope_2d(x):
    S, d = x.shape[1], x.shape[-1]
    t = jnp.arange(S)
    row, col = t // GRID_W, t % GRID_W
    return jnp.concatenate([_rope_axis(x[..., :d // 2], row), _rope_axis(x[..., d // 2:], col)], axis=-1)


def centred_shift(x, mu):
    xp = jnp.pad(x, ((0, 0), (1, 1), (0, 0)))
    return x + (0.5 * (xp[:, :-2] + xp[:, 2:]) - x) * mu


def flip_seq(ts):
    return tuple(jnp.flip(t, axis=1) for t in ts)


def bidirectional_scan(scan_fn, lat_f, ctx_f, lat_b, ctx_b, state0):
    yc_f, st_f = scan_fn(*ctx_f, state0)
    y_f, _ = scan_fn(*lat_f, st_f)
    yc_b, st_b = scan_fn(*flip_seq(ctx_b), state0)
    y_b, _ = scan_fn(*flip_seq(lat_b), st_b)
    return y_f + jnp.flip(y_b, axis=1), yc_f + jnp.flip(yc_b, axis=1)


def mlstm_chunkwise(q, k, v, i_pre, logf, state):
    B, S, H, DK = q.shape
    DV = v.shape[-1]
    L = ML_CHUNK
    nc = S // L
    qc = q.reshape(B, nc, L, H, DK).transpose(1, 0, 3, 2, 4)
    kc = k.reshape(B, nc, L, H, DK).transpose(1, 0, 3, 2, 4)
    vc = v.reshape(B, nc, L, H, DV).transpose(1, 0, 3, 2, 4)
    ic = i_pre.reshape(B, nc, L, H).transpose(1, 0, 3, 2)
    fc = logf.reshape(B, nc, L, H).transpose(1, 0, 3, 2)
    lower = jnp.tril(jnp.ones((L, L), dtype=bool))

    def step(carry, xs):
        C, n, m = carry
        qb, kb, vb, ib, fb = xs
        b = jnp.cumsum(fb, axis=-1)
        d_mat = jnp.where(lower, b[..., :, None] - b[..., None, :] + ib[..., None, :], -jnp.inf)
        m_inter = b + m[..., None]
        m_q = jnp.maximum(m_inter, jnp.max(d_mat, axis=-1))
        w_intra = jnp.exp(d_mat - m_q[..., None])
        w_inter = jnp.exp(m_inter - m_q)
        s = jnp.einsum('bhtd,bhsd->bhts', qb, kb) * w_intra
        num = jnp.einsum('bhts,bhsv->bhtv', s, vb) + w_inter[..., None] * jnp.einsum('bhtd,bhdv->bhtv', qb, C)
        den = jnp.sum(s, axis=-1) + w_inter * jnp.einsum('bhtd,bhd->bht', qb, n)
        h = num / jnp.maximum(jnp.abs(den), jnp.exp(-m_q))[..., None]
        b_last = b[..., -1]
        g = b_last[..., None] - b + ib
        m_new = jnp.maximum(b_last + m, jnp.max(g, axis=-1))
        wk = jnp.exp(g - m_new[..., None])
        carry_decay = jnp.exp(b_last + m - m_new)
        C = carry_decay[..., None, None] * C + jnp.einsum('bhs,bhsd,bhsv->bhdv', wk, kb, vb)
        n = carry_decay[..., None] * n + jnp.einsum('bhs,bhsd->bhd', wk, kb)
        return (C, n, m_new), h

    state, hs = lax.scan(step, state, (qc, kc, vc, ic, fc))
    return hs.transpose(1, 0, 3, 2, 4).reshape(B, S, H, DV), state


def mlstm_streams(blk, if_bias, rope):
    q, k, v, o, g = jnp.split(blk, ML_SPLITS, axis=-1)
    B, S, _ = q.shape
    q = q.reshape(B, S, ML_HEADS, ML_DQK)
    k = k.reshape(B, S, ML_HEADS, ML_DQK)
    if rope:
        q, k = rope_2d(q), rope_2d(k)
    q = q.astype(jnp.float32)
    k = k.astype(jnp.float32) * (ML_DQK ** -0.5)
    v = v.reshape(B, S, ML_HEADS, ML_DV).astype(jnp.float32)
    g = g.astype(jnp.float32) + if_bias
    i_f, i_b, f_f, f_b = jnp.split(g, 4, axis=-1)
    fwd = (q, k, v, i_f, jax.nn.log_sigmoid(f_f))
    bwd = (q, k, v, i_b, jax.nn.log_sigmoid(f_b))
    return fwd, bwd, o


def mlstm_output(h, o, norm_w):
    B, S = h.shape[:2]
    hn = h * lax.rsqrt(jnp.mean(h * h, axis=-1, keepdims=True) + NORM_EPS)
    return hn.reshape(B, S, ML_V) * norm_w * jax.nn.sigmoid(o.astype(jnp.float32))


def neighbourhood_attention(q, k, v, kc, vc, rpb):
    B, T, H, dh = q.shape
    rows = T // GRID_W
    kh = min(NA_WIN_ROWS, rows)
    scale = dh ** -0.5
    qg = q.reshape(B, rows, GRID_W, H, dh)
    kg = k.reshape(B, rows, GRID_W, H, dh)
    vg = v.reshape(B, rows, GRID_W, H, dh)
    cols = jnp.arange(GRID_W)
    cs = jnp.clip(cols - NA_WIN_COLS // 2, 0, GRID_W - NA_WIN_COLS)
    in_win = (cols[None, :] >= cs[:, None]) & (cols[None, :] < cs[:, None] + NA_WIN_COLS)
    col_idx = jnp.clip(cols[None, :] - cols[:, None], 1 - NA_WIN_COLS, NA_WIN_COLS - 1) + NA_WIN_COLS - 1

    def one_row(r):
        rs = jnp.clip(r - kh // 2, 0, rows - kh)
        q_r = lax.dynamic_index_in_dim(qg, r, axis=1, keepdims=False)
        k_r = lax.dynamic_slice_in_dim(kg, rs, kh, axis=1)
        v_r = lax.dynamic_slice_in_dim(vg, rs, kh, axis=1)
        row_idx = rs + jnp.arange(kh) - r + NA_WIN_ROWS - 1
        bias = rpb[:, row_idx[None, :, None], col_idx[:, None, :]].astype(jnp.float32)
        bias = jnp.where(in_win[None, :, None, :], bias, -jnp.inf)
        s_loc = jnp.einsum('bqhd,bikhd->bhqik', q_r, k_r).astype(jnp.float32) * scale + bias
        s_ctx = jnp.einsum('bqhd,bchd->bhqc', q_r, kc).astype(jnp.float32) * scale
        logits = jnp.concatenate([s_loc.reshape(B, H, GRID_W, kh * GRID_W), s_ctx], axis=-1)
        p = jax.nn.softmax(logits, axis=-1).astype(v.dtype)
        p_loc = p[..., :kh * GRID_W].reshape(B, H, GRID_W, kh, GRID_W)
        p_ctx = p[..., kh * GRID_W:]
        return jnp.einsum('bhqik,bikhd->bqhd', p_loc, v_r) + jnp.einsum('bhqc,bchd->bqhd', p_ctx, vc)

    out = lax.map(one_row, jnp.arange(rows))
    return jnp.moveaxis(out, 0, 1).reshape(B, T, H, dh)


def context_attention(qc, kc, vc):
    s = jnp.einsum('bqhd,bkhd->bhqk', qc, kc).astype(jnp.float32) * (qc.shape[-1] ** -0.5)
    p = jax.nn.softmax(s, axis=-1).astype(vc.dtype)
    return jnp.einsum('bhqk,bkhd->bqhd', p, vc)


def rwkv7_scan(r, w, k, v, kk, a, state):
    def step(S, xs):
        r_t, w_t, k_t, v_t, kk_t, a_t = xs
        s_kk = jnp.einsum('bhvk,bhk->bhv', S, kk_t)
        S = (S * w_t[:, :, None, :] - s_kk[..., None] * (kk_t * a_t)[:, :, None, :]
             + v_t[..., None] * k_t[:, :, None, :])
        return S, jnp.einsum('bhvk,bhk->bhv', S, r_t)

    xs = tuple(jnp.moveaxis(t, 1, 0) for t in (r, w, k, v, kk, a))
    state, ys = lax.scan(step, state, xs)
    return jnp.moveaxis(ys, 0, 1), state


def rwkv7_streams(blk, lp):
    blk = blk.astype(jnp.float32)
    B, S, _ = blk.shape
    r, k, v, wd, ad, gd = jnp.split(blk, RW_SPLITS, axis=-1)

    def heads(t):
        return t.reshape(B, S, RW_HEADS, RW_N)

    kk = heads(k * lp['rw_k_k'])
    kk = kk / jnp.maximum(jnp.sqrt(jnp.sum(kk * kk, axis=-1, keepdims=True)), 1e-12)
    dirs = []
    for d in range(2):
        w_log = -jax.nn.softplus(-(lp['rw_w0'][d] + jnp.tanh(wd) @ lp['rw_w_up'][d])) - 0.5
        decay = jnp.exp(-jnp.exp(w_log))
        a = jax.nn.sigmoid(lp['rw_a0'][d] + ad @ lp['rw_a_up'][d])
        k_mod = k * (1 + (a - 1) * lp['rw_k_a'])
        dirs.append((heads(r), heads(decay), heads(k_mod), heads(v), kk, heads(a)))
    g = jax.nn.sigmoid(gd) @ lp['rw_g_up']
    return dirs[0], dirs[1], g


def rwkv7_output(y, fwd, bwd, g, lp):
    B, S = y.shape[:2]
    r, v = fwd[0], fwd[3]
    k_bonus = 0.5 * (fwd[2] + bwd[2])
    yc = y - jnp.mean(y, axis=-1, keepdims=True)
    yn = yc * lax.rsqrt(jnp.mean(yc * yc, axis=-1, keepdims=True) + RW_GN_EPS)
    yn = yn.reshape(B, S, RW_W) * lp['rw_ln_w'] + lp['rw_ln_b']
    bonus = (jnp.sum(r * k_bonus * lp['rw_r_k'], axis=-1, keepdims=True) * v).reshape(B, S, RW_W)
    return (yn + bonus) * g


def hybrid_mixers(u, uc, lp, need_ctx):
    B, T, _ = u.shape
    C = uc.shape[1]
    P = u @ lp['w_in']
    Pc = uc @ lp['w_in']
    ml, na, rw = jnp.split(P, (ML_COLS, ML_COLS + NA_IN), axis=-1)
    mlc, nac, rwc = jnp.split(Pc, (ML_COLS, ML_COLS + NA_IN), axis=-1)

    ml_f, ml_b, ml_o = mlstm_streams(ml, lp['ml_if_bias'], True)
    mlc_f, mlc_b, mlc_o = mlstm_streams(mlc, lp['ml_if_bias'], False)
    zero_ml = (jnp.zeros((B, ML_HEADS, ML_DQK, ML_DV), jnp.float32),
               jnp.zeros((B, ML_HEADS, ML_DQK), jnp.float32),
               jnp.zeros((B, ML_HEADS), jnp.float32))
    h_ml, hc_ml = bidirectional_scan(mlstm_chunkwise, ml_f, mlc_f, ml_b, mlc_b, zero_ml)
    y_ml = mlstm_output(h_ml, ml_o, lp['ml_norm_w'])

    nq, nk, nv = [t.reshape(B, T, NA_HEADS, NA_DH) for t in jnp.split(na, 3, axis=-1)]
    ncq, nck, ncv = [t.reshape(B, C, NA_HEADS, NA_DH) for t in jnp.split(nac, 3, axis=-1)]
    y_na = neighbourhood_attention(nq, nk, nv, nck, ncv, lp['na_rpb']).reshape(B, T, NA_W)

    rw_f, rw_b, rw_g = rwkv7_streams(centred_shift(rw, lp['rw_mu']), lp)
    rwc_f, rwc_b, rwc_g = rwkv7_streams(centred_shift(rwc, lp['rw_mu']), lp)
    zero_rw = jnp.zeros((B, RW_HEADS, RW_N, RW_N), jnp.float32)
    s_rw, sc_rw = bidirectional_scan(rwkv7_scan, rw_f, rwc_f, rw_b, rwc_b, zero_rw)
    y_rw = rwkv7_output(s_rw, rw_f, rw_b, rw_g, lp)

    y = jnp.concatenate([y_ml.astype(u.dtype), y_na, y_rw.astype(u.dtype)], axis=-1)
    if not need_ctx:
        return y, None
    yc = jnp.concatenate([
        mlstm_output(hc_ml, mlc_o, lp['ml_norm_w']).astype(uc.dtype),
        context_attention(ncq, nck, ncv).reshape(B, C, NA_W),
        rwkv7_output(sc_rw, rwc_f, rwc_b, rwc_g, lp).astype(uc.dtype)], axis=-1)
    return y, yc


def sq_relu_mlp(u, w_up, w_down):
    return jnp.square(jax.nn.relu(u @ w_up)) @ w_down


def setup_inputs(seed: int = 0) -> dict:
    key = jax.random.key(seed)
    ks = jax.random.split(key, 32)
    f32 = jnp.float32
    L, D = DEPTH, D_MODEL

    def nrm(k, shape, scale):
        return jax.random.normal(k, shape, f32) * scale

    ib = nrm(ks[9], (L, 2 * ML_HEADS), 0.1)
    fb = jnp.tile(jnp.linspace(3.0, 6.0, ML_HEADS), (L, 2)) + nrm(ks[10], (L, 2 * ML_HEADS), 0.1)
    return {
        'x': nrm(ks[0], (BATCH, SEQ, D), 1.0),
        'c': nrm(ks[1], (BATCH, D), 1.0),
        'ctx': nrm(ks[2], (BATCH, CTX_LEN, D), 1.0),
        'c_ctx': nrm(ks[3], (D,), 1.0),
        'mod_w': nrm(ks[4], (L, D, 6 * D), 0.5 * D ** -0.5),
        'mod_b': nrm(ks[5], (L, 6 * D), 0.02),
        'norm1_w': 1.0 + nrm(ks[6], (L, D), 0.02),
        'norm2_w': 1.0 + nrm(ks[7], (L, D), 0.02),
        'w_in': nrm(ks[8], (L, D, IN_COLS), D ** -0.5),
        'ml_if_bias': jnp.concatenate([ib, fb], axis=-1),
        'ml_norm_w': 1.0 + nrm(ks[11], (L, ML_V), 0.02),
        'na_rpb': nrm(ks[12], (L, NA_HEADS, 2 * NA_WIN_ROWS - 1, 2 * NA_WIN_COLS - 1), 0.1),
        'rw_mu': jax.random.uniform(ks[13], (L, RW_COLS), f32, 0.0, 1.0),
        'rw_w0': jnp.linspace(-5.0, 1.0, RW_W)[None, None, :] + nrm(ks[14], (L, 2, RW_W), 0.1),
        'rw_w_up': nrm(ks[15], (L, 2, RW_DECAY_RANK, RW_W), 0.5 * RW_DECAY_RANK ** -0.5),
        'rw_a0': nrm(ks[16], (L, 2, RW_W), 0.1),
        'rw_a_up': nrm(ks[17], (L, 2, RW_A_RANK, RW_W), 0.5 * RW_A_RANK ** -0.5),
        'rw_g_up': nrm(ks[18], (L, RW_GATE_RANK, RW_W), RW_GATE_RANK ** -0.5),
        'rw_k_k': 0.85 + nrm(ks[19], (L, RW_W), 0.02),
        'rw_k_a': 1.0 + nrm(ks[20], (L, RW_W), 0.02),
        'rw_r_k': nrm(ks[21], (L, RW_HEADS, RW_N), 0.1),
        'rw_ln_w': 1.0 + nrm(ks[22], (L, RW_W), 0.02),
        'rw_ln_b': nrm(ks[23], (L, RW_W), 0.02),
        'w_out': nrm(ks[24], (L, MIX_W, D), MIX_W ** -0.5),
        'w_mlp_up': nrm(ks[25], (L, D, MLP_HIDDEN), D ** -0.5),
        'w_mlp_down': nrm(ks[26], (L, MLP_HIDDEN, D), MLP_HIDDEN ** -0.5),
        'final_norm_w': 1.0 + nrm(ks[27], (D,), 0.02),
    }


def reference(x, c, ctx, c_ctx, mod_w, mod_b, norm1_w, norm2_w, w_in, ml_if_bias, ml_norm_w, na_rpb,
              rw_mu, rw_w0, rw_w_up, rw_a0, rw_a_up, rw_g_up, rw_k_k, rw_k_a, rw_r_k, rw_ln_w, rw_ln_b,
              w_out, w_mlp_up, w_mlp_down, final_norm_w):
    h = x
    hc = ctx
    silu_c = jax.nn.silu(c)
    silu_cc = jax.nn.silu(c_ctx)
    for l in range(DEPTH):
        need_ctx = l < DEPTH - 1
        lp = {'w_in': w_in[l], 'ml_if_bias': ml_if_bias[l], 'ml_norm_w': ml_norm_w[l], 'na_rpb': na_rpb[l],
              'rw_mu': rw_mu[l], 'rw_w0': rw_w0[l], 'rw_w_up': rw_w_up[l], 'rw_a0': rw_a0[l],
              'rw_a_up': rw_a_up[l], 'rw_g_up': rw_g_up[l], 'rw_k_k': rw_k_k[l], 'rw_k_a': rw_k_a[l],
              'rw_r_k': rw_r_k[l], 'rw_ln_w': rw_ln_w[l], 'rw_ln_b': rw_ln_b[l]}
        mod = silu_c @ mod_w[l] + mod_b[l]
        modc = silu_cc @ mod_w[l] + mod_b[l]
        sh1, sc1, g1, sh2, sc2, g2 = jnp.split(mod[:, None, :], 6, axis=-1)
        csh1, csc1, cg1, csh2, csc2, cg2 = jnp.split(modc, 6, axis=-1)

        u = modulate(rms_norm(h, norm1_w[l]), sh1, sc1)
        uc = modulate(rms_norm(hc, norm1_w[l]), csh1, csc1)
        y, yc = hybrid_mixers(u, uc, lp, need_ctx)
        h = h + g1 * (y @ w_out[l])
        h = h + g2 * sq_relu_mlp(modulate(rms_norm(h, norm2_w[l]), sh2, sc2), w_mlp_up[l], w_mlp_down[l])
        if need_ctx:
            hc = hc + cg1 * (yc @ w_out[l])
            hc = hc + cg2 * sq_relu_mlp(modulate(rms_norm(hc, norm2_w[l]), csh2, csc2),
                                        w_mlp_up[l], w_mlp_down[l])
    return rms_norm(h, final_norm_w)
```

```python
import numpy as np
import concourse.bass as bass
import concourse.mybir as mybir
from concourse.bass_utils import run_bass_kernel_spmd

F32 = mybir.dt.float32
BF16 = mybir.dt.bfloat16
ALU = mybir.AluOpType
AF = mybir.ActivationFunctionType
AX = mybir.AxisListType


class K:
    def __init__(self, n_dma_sems=24):
        self.nc = bass.Bass("TRN2", target_bir_lowering=False)
        nc = self.nc
        self.eng = {"pe": nc.tensor, "act": nc.scalar, "dve": nc.vector, "pool": nc.gpsimd, "sp": nc.sync}
        self.sem = {e: nc.alloc_semaphore(name=f"sem_{e}") for e in self.eng}
        self.cnt = {e: 0 for e in self.eng}
        self.waited = {e: {} for e in self.eng}
        self.dma_sems = [nc.alloc_semaphore(name=f"dsem{i}") for i in range(n_dma_sems)]
        self.dma_cnt = [0] * n_dma_sems
        self.dma_rr = 0
        self.last_write = {}
        self.reads = {}
        self.semobj = {}
        for e in self.eng:
            self.semobj[("e", e)] = self.sem[e]
        for i, s in enumerate(self.dma_sems):
            self.semobj[("d", i)] = s
        self.out_dma = []
        self.ninst = 0

    def _need(self, reads, writes):
        need = {}

        def add(sv):
            if sv is None:
                return
            s, v = sv
            if need.get(s, 0) < v:
                need[s] = v
        for r in reads:
            add(self.last_write.get(r))
        for w in writes:
            add(self.last_write.get(w))
            for s, v in self.reads.get(w, {}).items():
                add((s, v))
        return need

    def _emit_waits(self, e, need):
        eng = self.eng[e]
        wd = self.waited[e]
        for s, v in need.items():
            if e == "pe" and s == ("e", "pe"):
                continue
            if wd.get(s, 0) < v:
                eng.wait_ge(self.semobj[s], v)
                wd[s] = v
                self.ninst += 1

    def _record(self, sv, reads, writes):
        s, v = sv
        for r in reads:
            d = self.reads.setdefault(r, {})
            if d.get(s, 0) < v:
                d[s] = v
        for w in writes:
            self.last_write[w] = sv
            self.reads[w] = {}

    def op(self, e, fn, reads=(), writes=(), inc=True):
        need = self._need(reads, writes)
        self._emit_waits(e, need)
        ins = fn(self.eng[e])
        self.ninst += 1
        if inc:
            self.cnt[e] += 1
            ins.then_inc(self.sem[e], 1)
            self._record((("e", e), self.cnt[e]), reads, writes)
        else:
            self._record((("e", e), self.cnt[e] + 1), reads, writes)
        return ins

    def dma(self, q, out, in_, reads=(), writes=(), slot=None, is_output=False, **kw):
        need = self._need(reads, writes)
        self._emit_waits(q, need)
        if slot is None:
            slot = self.dma_rr
            self.dma_rr = (self.dma_rr + 1) % len(self.dma_sems)
        if self.dma_cnt[slot] > 0:
            self._emit_waits(q, {("d", slot): self.dma_cnt[slot]})
        ins = self.eng[q].dma_start(out=out, in_=in_, **kw)
        self.dma_cnt[slot] += 16
        ins.then_inc(self.dma_sems[slot], 16)
        self.ninst += 1
        sv = (("d", slot), self.dma_cnt[slot])
        self._record(sv, reads, writes)
        if is_output:
            self.out_dma.append(sv)
        return ins

    def finish(self):
        need = {}
        for s, v in self.out_dma:
            if need.get(s, 0) < v:
                need[s] = v
        self._emit_waits("sp", need)
        return self.nc


D = 4096
T = 8192
CTX = 256
TOK = T + CTX
NTILE = TOK // 128
DEPTH = 2
GRID_W = 64
ML_H, ML_DQK, ML_DV = 6, 128, 256
NA_H, NA_DH = 8, 128
RW_H, RW_N = 24, 64
ML_QK = 768
ML_V = 1536
NA_W = 1024
RW_W = 1536
ML_COLS = 2 * ML_QK + 2 * ML_V + 24
NA_IN = 3 * NA_W
RW_COLS = 3 * RW_W + 128 + 128 + 480
IN_COLS = ML_COLS + NA_IN + RW_COLS
HID = 4 * D
EPS = 1e-6
C_NAQ = ML_COLS
C_NAK = ML_COLS + NA_W
C_NAV = ML_COLS + 2 * NA_W
C_RW = ML_COLS + NA_IN


def _ap(t):
    return t.ap() if hasattr(t, "ap") and callable(t.ap) else t[:]


class Prog(K):
    def __init__(self):
        super().__init__()
        nc = self.nc
        self.ps = [nc.alloc_psum_tensor(f"psb{i}", [128, 512], F32).ap() for i in range(8)]
        self.ps_rr = 0
        self._sb_stack = []

    def uname(self, name):
        self._uid = getattr(self, "_uid", 0) + 1
        return f"{name}_u{self._uid}"

    def psum(self):
        i = self.ps_rr
        self.ps_rr = (self.ps_rr + 1) % 8
        return i, self.ps[i]

    def dram(self, name, shape, dt=F32, kind="Internal"):
        return self.nc.dram_tensor(name, list(shape), dt, kind=kind).ap()

    def barrier(self):
        need = {("e", e): c for e, c in self.cnt.items() if c > 0}
        for i, c in enumerate(self.dma_cnt):
            if c > 0:
                need[("d", i)] = c
        for e in self.eng:
            self._emit_waits(e, need)
        self.last_write = {}
        self.reads = {}


def vec_bcast(ap1d, parts):
    return ap1d.partition_broadcast(parts)


def emit_norm_tile(p, xt, usq, ss, rstd, gsc, sh, ubf, tmp):
    p.op("act", lambda e: e.activation(out=usq, in_=xt, func=AF.Square, accum_out=ss), reads=["xt"], writes=["usq", "ss"])
    p.op("dve", lambda e: e.tensor_scalar(out=rstd, in0=ss, scalar1=1.0 / D, scalar2=EPS, op0=ALU.mult, op1=ALU.add),
         reads=["ss"], writes=["rstd"])
    p.op("act", lambda e: e.activation(out=rstd, in_=rstd, func=AF.Sqrt), reads=["rstd"], writes=["rstd"])
    p.op("dve", lambda e: e.reciprocal(out=rstd, in_=rstd), reads=["rstd"], writes=["rstd"])
    p.op("dve", lambda e: e.scalar_tensor_tensor(out=tmp, in0=xt, scalar=rstd, in1=gsc, op0=ALU.mult, op1=ALU.mult),
         reads=["xt", "rstd", "gsc"], writes=["tmp"])
    p.op("dve", lambda e: e.tensor_tensor(out=ubf, in0=tmp, in1=sh, op=ALU.add), reads=["tmp", "sh"], writes=["ubf"])


def emit_transpose_tile(p, ubf, uT, col0, identb, nk=32):
    for g in range(nk // 8):
        bi, ps = p.psum()
        psb = ps.bitcast(BF16)
        for j in range(8):
            kc = g * 8 + j
            p.op("pe", lambda e: e.transpose(out=psb[:, j * 128:(j + 1) * 128], in_=ubf[:, kc * 128:(kc + 1) * 128], identity=identb),
                 reads=["ubf", "identb"], writes=[("ps", bi)])
        p.op("act", lambda e: e.activation(out=uT[:, g * 8:(g + 1) * 8, col0:col0 + 128],
                                           in_=psb.rearrange("p (j t) -> p j t", j=8), func=AF.Copy),
             reads=[("ps", bi)], writes=["uT"])


from contextlib import ExitStack


def col_blocks_A():
    blocks = []

    def add(c0, c1, mode, dst):
        c = c0
        while c < c1:
            w = min(512, c1 - c)
            blocks.append((c, w, mode, dst, c - c0))
            c += w
    add(0, ML_COLS, "tm", "ml")
    add(C_NAQ, C_NAV, "fm", "naqk")
    add(C_NAV, C_RW, "tm", "nav")
    add(C_RW, IN_COLS, "tm", "rw")
    return blocks


def alloc_P(p):
    return {"ml": p.dram("P_ml", [TOK, ML_COLS]), "nav": p.dram("P_nav", [TOK, NA_W]),
            "naqk": p.dram("PT_naqk", [2 * NA_W, TOK]), "rw": p.dram("P_rw", [TOK, RW_COLS])}


NBLK_A = 2 * len(col_blocks_A())
NBLK_C = 16 + 64 + 64


class _Chunks:
    def __init__(self, p, name, nblk, per=64):
        self.per = per
        self.t = [[p.dram(f"{name}_{l}_{g}", [min(per, nblk - g * per), 128, 8192], BF16) for g in range((nblk + per - 1) // per)]
                  for l in range(DEPTH)]

    def __getitem__(self, lb):
        l, blk = lb
        return self.t[l][blk // self.per][blk % self.per]


def alloc_wbf(p):
    return {"A": _Chunks(p, "wbfA", NBLK_A), "C": _Chunks(p, "wbfC", NBLK_C)}


def stage_precast(p, layers, w_in, w_out, w_up, w_down, WB, only=None):
    nc = p.nc
    with ExitStack() as es:
        def sb(name, shape, dt):
            return _ap(es.enter_context(nc.sbuf_tensor(p.uname(name), shape, dt)))
        NB = 6
        st = [sb(f"pc{i}", [128, 16, 512], BF16) for i in range(NB)]
        jobs = []
        for l in layers:
            if only in (None, "A"):
                wi = w_in[l].rearrange("(k p) n -> p k n", p=128)
                for i, (c0, w, mode, dn, lc0) in enumerate(col_blocks_A()):
                    for q in range(2):
                        jobs.append((wi[:, q * 16:(q + 1) * 16, c0:c0 + w], w, WB["A"][l, 2 * i + q]))
            if only in (None, "C"):
                wo = w_out[l].rearrange("(k p) n -> p k n", p=128)
                wu = w_up[l].rearrange("(k p) n -> p k n", p=128)
                wd = w_down[l].rearrange("(k p) n -> p k n", p=128)
                for n in range(8):
                    for q in range(2):
                        jobs.append((wo[:, q * 16:(q + 1) * 16, n * 512:(n + 1) * 512], 512, WB["C"][l, n * 2 + q]))
                for hb in range(32):
                    for q in range(2):
                        jobs.append((wu[:, q * 16:(q + 1) * 16, hb * 512:(hb + 1) * 512], 512, WB["C"][l, 16 + hb * 2 + q]))
                for n in range(8):
                    for q in range(8):
                        jobs.append((wd[:, q * 16:(q + 1) * 16, n * 512:(n + 1) * 512], 512, WB["C"][l, 80 + n * 8 + q]))
        for j, (src, w, dst) in enumerate(jobs):
            b = st[j % NB]
            p.dma("pool", b[:, :, 0:w], src, writes=[("pc", j % NB)])
            p.dma("sp" if j % 2 == 0 else "act", dst.rearrange("p (k n) -> p k n", n=512)[:, :, 0:w], b[:, :, 0:w],
                  reads=[("pc", j % NB)], writes=["wbf"])
    p.barrier()


def stage_mod(p, cvec, mod_w, mod_b, modv, ident_d):
    nc = p.nc
    with ExitStack() as es:
        def sb(name, shape, dt):
            return _ap(es.enter_context(nc.sbuf_tensor(p.uname(name), shape, dt)))
        identf = sb("identf", [128, 128], F32)
        crow = sb("crow", [2, D], F32)
        sT = sb("sT", [128, 64], F32)
        Wm = [sb(f"Wm{i}", [128, 32, 512], F32) for i in range(2)]
        mb = sb("mb", [2, 512], F32)
        mrow = [sb(f"mrow{i}", [2, 512], F32) for i in range(2)]
        p.dma("sp", identf, ident_d, writes=["identf"])
        p.dma("sp", crow, cvec, writes=["crow"])
        p.op("act", lambda e: e.activation(out=crow, in_=crow, func=AF.Silu), reads=["crow"], writes=["crow"])
        bi, ps = p.psum()
        for kc in range(32):
            p.op("pe", lambda e: e.transpose(out=ps[:, kc * 2:(kc + 1) * 2], in_=crow[0:2, kc * 128:(kc + 1) * 128],
                                             identity=identf[0:2, 0:2]), reads=["crow", "identf"], writes=[("ps", bi)])
        p.op("dve", lambda e: e.tensor_copy(out=sT, in_=ps[:, 0:64]), reads=[("ps", bi)], writes=["sT"])
        sT3 = sT.rearrange("p (k r) -> p k r", r=2)
        nblk = 6 * D // 512
        jobs = [(l, n) for l in range(DEPTH) for n in range(nblk)]

        def load(i):
            l, n = jobs[i]
            p.dma("sp" if i % 2 == 0 else "act", Wm[i % 2], mod_w[l, :, n * 512:(n + 1) * 512].rearrange("(k p) n -> p k n", p=128),
                  writes=[("Wm", i % 2)])
        load(0)
        for i, (l, n) in enumerate(jobs):
            if i + 1 < len(jobs):
                load(i + 1)
            p.dma("sp", mb, mod_b[l, n * 512:(n + 1) * 512].partition_broadcast(2), writes=["mb"])
            bi, ps = p.psum()
            for kc in range(32):
                p.op("pe", lambda e: e.matmul(ps[0:2, :], lhsT=sT3[:, kc, :], rhs=Wm[i % 2][:, kc, :], start=(kc == 0), stop=(kc == 31)),
                     reads=["sT", ("Wm", i % 2)], writes=[("ps", bi)])
            mr = mrow[i % 2]
            p.op("dve", lambda e: e.tensor_tensor(out=mr, in0=ps[0:2, :], in1=mb, op=ALU.add),
                 reads=[("ps", bi), "mb"], writes=[("mrow", i % 2)])
            p.dma("sp", modv[l, :, n * 512:(n + 1) * 512], mr, reads=[("mrow", i % 2)], writes=[("modv", l)])
    p.barrier()


def load_mod_tiles(p, modv, l, row, i_sh, i_sc, norm_w_l, gsc, sh, tmp):
    p.dma("sp", tmp, norm_w_l.partition_broadcast(128), writes=["tmp"])
    p.dma("sp", gsc, modv[l, row, i_sc * D:(i_sc + 1) * D].partition_broadcast(128), reads=[("modv", l)], writes=["gsc"])
    p.dma("sp", sh, modv[l, row, i_sh * D:(i_sh + 1) * D].partition_broadcast(128), reads=[("modv", l)], writes=["sh"])
    p.op("dve", lambda e: e.scalar_tensor_tensor(out=gsc, in0=gsc, scalar=1.0, in1=tmp, op0=ALU.add, op1=ALU.mult),
         reads=["gsc", "tmp"], writes=["gsc"])


def stage_A(p, l, hbuf, modv, norm1_w, WB, PD, ident_d, blocks_tok=None):
    nc = p.nc
    cbs = col_blocks_A()
    if blocks_tok is None:
        blocks_tok = [(0, 2, 1)] + [(2 + 4 * i, 4, 0) for i in range(16)]
    with ExitStack() as es:
        def sb(name, shape, dt):
            return _ap(es.enter_context(nc.sbuf_tensor(p.uname(name), shape, dt)))
        identb = sb("identb", [128, 128], BF16)
        gsc = sb("gsc", [128, D], F32)
        sh = sb("sh", [128, D], F32)
        xt = sb("xt", [128, D], F32)
        tmp = sb("tmp", [128, D], F32)
        ubf = sb("ubf", [128, D], BF16)
        uT = sb("uT", [128, 32, 512], BF16)
        Wb = [sb(f"Wb{i}", [128, 32, 512], BF16) for i in range(2)]
        ob = [sb(f"ob{i}", [128, 512], F32) for i in range(4)]
        ss = sb("ss", [128, 1], F32)
        rstd = sb("rstd", [128, 1], F32)
        p.dma("pool", identb, ident_d, writes=["identb"])
        cur_row = None
        orr = 0
        for (t0, nt, row) in blocks_tok:
            if row != cur_row:
                load_mod_tiles(p, modv, l, row, 0, 1, norm1_w[l], gsc, sh, tmp)
                cur_row = row
            ntok = nt * 128
            for j in range(nt):
                p.dma("sp", xt, hbuf[(t0 + j) * 128:(t0 + j + 1) * 128, :], reads=[("hbuf", t0 + j)], writes=["xt"])
                emit_norm_tile(p, xt, tmp, ss, rstd, gsc, sh, ubf, tmp)
                emit_transpose_tile(p, ubf, uT, j * 128, identb)

            def loadW(i):
                w = cbs[i][1]
                for q in range(2):
                    p.dma("sp" if q == 0 else "pool", Wb[i % 2][:, q * 16:(q + 1) * 16, 0:w],
                          WB["A"][l, 2 * i + q].rearrange("p (k n) -> p k n", n=512)[:, :, 0:w], reads=["wbf"], writes=[("Wb", i % 2, q)])
            loadW(0)
            for i, (c0, w, mode, dn, lc0) in enumerate(cbs):
                if i + 1 < len(cbs):
                    loadW(i + 1)
                W = Wb[i % 2]
                if mode == "tm":
                    for j in range(nt):
                        bi, ps = p.psum()
                        for kc in range(32):
                            p.op("pe", lambda e: e.matmul(ps[:, 0:w], lhsT=uT[:, kc, j * 128:(j + 1) * 128], rhs=W[:, kc, 0:w],
                                                          start=(kc == 0), stop=(kc == 31)),
                                 reads=["uT", ("Wb", i % 2, kc // 16)], writes=[("ps", bi)])
                        o = ob[orr % 4]
                        ok = ("ob", orr % 4)
                        orr += 1
                        p.op("act", lambda e: e.activation(out=o[:, 0:w], in_=ps[:, 0:w], func=AF.Copy), reads=[("ps", bi)], writes=[ok])
                        p.dma("sp", PD[dn][(t0 + j) * 128:(t0 + j + 1) * 128, lc0:lc0 + w], o[:, 0:w], reads=[ok], writes=[("P", t0 + j)])
                else:
                    for s in range((w + 127) // 128):
                        ws = min(128, w - s * 128)
                        bi, ps = p.psum()
                        for kc in range(32):
                            p.op("pe", lambda e: e.matmul(ps[0:ws, 0:ntok], lhsT=W[:, kc, s * 128:s * 128 + ws], rhs=uT[:, kc, 0:ntok],
                                                          start=(kc == 0), stop=(kc == 31)),
                                 reads=["uT", ("Wb", i % 2, kc // 16)], writes=[("ps", bi)])
                        o = ob[orr % 4]
                        ok = ("ob", orr % 4)
                        orr += 1
                        p.op("act", lambda e: e.activation(out=o[0:ws, 0:ntok], in_=ps[0:ws, 0:ntok], func=AF.Copy),
                             reads=[("ps", bi)], writes=[ok])
                        p.dma("sp", PD[dn][lc0 + s * 128:lc0 + s * 128 + ws, t0 * 128:t0 * 128 + ntok], o[0:ws, 0:ntok],
                              reads=[ok], writes=[("P", t0 + jj) for jj in range(nt)])
    p.barrier()


def stage_C(p, l, hbuf, y, modv, norm2_w, WB, ident_d, last, final_w=None, out=None, blocks_tok=None):
    nc = p.nc
    if blocks_tok is None:
        blocks_tok = ([] if last else [(0, 1)]) + [(2 + 2 * i, 0) for i in range(32)]
    KP = 16
    with ExitStack() as es:
        def sb(name, shape, dt):
            return _ap(es.enter_context(nc.sbuf_tensor(p.uname(name), shape, dt)))
        identb = sb("identb", [128, 128], BF16)
        gsc = sb("gsc", [128, D], F32)
        sh = sb("sh", [128, D], F32)
        tmp = sb("tmp", [128, D], F32)
        hres = [sb(f"hres{i}", [128, D], F32) for i in range(2)]
        ubf = sb("ubf", [128, D], BF16)
        uT = sb("uT", [128, 32, 256], BF16)
        hidT = sb("hidT", [128, 128, 256], BF16)
        NWB = 2
        Wb = [sb(f"Wb{i}", [128, KP, 512], BF16) for i in range(NWB)]
        gb = [sb(f"gb{i}", [128, 512], F32) for i in range(2)]
        rl = [sb(f"rl{i}", [128, 256], F32) for i in range(2)]
        ss = sb("ss", [128, 1], F32)
        rstd = sb("rstd", [128, 1], F32)
        rs2 = sb("rs2", [128, 2], F32)
        p.dma("pool", identb, ident_d, writes=["identb"])
        cur_row = None
        wrr = [0]
        grr = [0]

        def loadW(blk):
            i = wrr[0] % NWB
            wrr[0] += 1
            src = WB["C"][l, blk].rearrange("p (k n) -> p k n", n=512)
            p.dma("sp", Wb[i][:, 0:KP // 2, :], src[:, 0:KP // 2, :], reads=["wbf"], writes=[("Wb", i, 0)])
            p.dma("pool", Wb[i][:, KP // 2:KP, :], src[:, KP // 2:KP, :], reads=["wbf"], writes=[("Wb", i, 1)])
            return i

        def loadg(row, gi, c0):
            i = grr[0] % 2
            grr[0] += 1
            p.dma("sp", gb[i], modv[l, row, gi * D + c0:gi * D + c0 + 512].partition_broadcast(128), reads=[("modv", l)], writes=[("gb", i)])
            return i

        for (t0, row) in blocks_tok:
            if row != cur_row:
                load_mod_tiles(p, modv, l, row, 3, 4, norm2_w[l], gsc, sh, tmp)
                cur_row = row
            for j in range(2):
                p.dma("pool", ubf, y[(t0 + j) * 128:(t0 + j + 1) * 128, :], reads=[("y", t0 + j)], writes=["ubf"])
                emit_transpose_tile(p, ubf, uT, j * 128, identb)
                p.dma("sp", hres[j], hbuf[(t0 + j) * 128:(t0 + j + 1) * 128, :], reads=[("hbuf", t0 + j)], writes=[("hres", j)])
            for n in range(8):
                gi = loadg(row, 2, n * 512)
                banks = [p.psum() for _ in range(2)]
                for q in range(32 // KP):
                    wi = loadW(n * 2 + q)
                    for j in range(2):
                        bi, ps = banks[j]
                        for kk in range(KP):
                            kc = q * KP + kk
                            p.op("pe", lambda e: e.matmul(ps, lhsT=uT[:, kc, j * 128:(j + 1) * 128], rhs=Wb[wi][:, kk, :],
                                                          start=(kc == 0), stop=(kc == 31)),
                                 reads=["uT", ("Wb", wi, kk // (KP // 2))], writes=[("ps", bi)])
                for j in range(2):
                    bi, ps = banks[j]
                    hs = hres[j][:, n * 512:(n + 1) * 512]
                    p.op("dve", lambda e: e.tensor_tensor(out=tmp[:, 0:512], in0=ps, in1=gb[gi], op=ALU.mult),
                         reads=[("ps", bi), ("gb", gi)], writes=["tmp"])
                    p.op("dve", lambda e: e.tensor_tensor(out=hs, in0=hs, in1=tmp[:, 0:512], op=ALU.add),
                         reads=["tmp", ("hres", j)], writes=[("hres", j)])
            for j in range(2):
                p.op("act", lambda e: e.activation(out=ubf, in_=hres[j], func=AF.Square, accum_out=ss), reads=[("hres", j)], writes=["ubf", "ss"])
                p.op("dve", lambda e: e.tensor_scalar(out=rstd, in0=ss, scalar1=1.0 / D, scalar2=EPS, op0=ALU.mult, op1=ALU.add),
                     reads=["ss"], writes=["rstd"])
                p.op("act", lambda e: e.activation(out=rstd, in_=rstd, func=AF.Sqrt), reads=["rstd"], writes=["rstd"])
                p.op("dve", lambda e: e.reciprocal(out=rstd, in_=rstd), reads=["rstd"], writes=["rstd"])
                p.op("dve", lambda e: e.scalar_tensor_tensor(out=tmp, in0=hres[j], scalar=rstd, in1=gsc, op0=ALU.mult, op1=ALU.mult),
                     reads=[("hres", j), "rstd", "gsc"], writes=["tmp"])
                p.op("dve", lambda e: e.tensor_tensor(out=ubf, in0=tmp, in1=sh, op=ALU.add), reads=["tmp", "sh"], writes=["ubf"])
                emit_transpose_tile(p, ubf, uT, j * 128, identb)
            rr = 0
            for hb in range(HID // 512):
                banks = [p.psum() for _ in range(4)]
                for q in range(32 // KP):
                    wi = loadW(16 + hb * 2 + q)
                    for s in range(4):
                        bi, ps = banks[s]
                        for kk in range(KP):
                            kc = q * KP + kk
                            p.op("pe", lambda e: e.matmul(ps[:, 0:256], lhsT=Wb[wi][:, kk, s * 128:(s + 1) * 128], rhs=uT[:, kc, :],
                                                          start=(kc == 0), stop=(kc == 31)),
                                 reads=["uT", ("Wb", wi, kk // (KP // 2))], writes=[("ps", bi)])
                for s in range(4):
                    bi, ps = banks[s]
                    r = rl[rr % 2]
                    rk = ("rl", rr % 2)
                    rr += 1
                    p.op("act", lambda e: e.activation(out=r, in_=ps[:, 0:256], func=AF.Relu), reads=[("ps", bi)], writes=[rk])
                    p.op("dve", lambda e: e.tensor_tensor(out=hidT[:, hb * 4 + s, :], in0=r, in1=r, op=ALU.mult), reads=[rk], writes=["hidT"])
            for n in range(8):
                gi = loadg(row, 5, n * 512)
                banks = [p.psum() for _ in range(2)]
                nq = 128 // KP
                for q in range(nq):
                    wi = loadW(80 + n * 8 + q)
                    for j in range(2):
                        bi, ps = banks[j]
                        for kk in range(KP):
                            kc = q * KP + kk
                            p.op("pe", lambda e: e.matmul(ps, lhsT=hidT[:, kc, j * 128:(j + 1) * 128], rhs=Wb[wi][:, kk, :],
                                                          start=(kc == 0), stop=(kc == 127)),
                                 reads=["hidT", ("Wb", wi, kk // (KP // 2))], writes=[("ps", bi)])
                for j in range(2):
                    bi, ps = banks[j]
                    hs = hres[j][:, n * 512:(n + 1) * 512]
                    p.op("dve", lambda e: e.tensor_tensor(out=tmp[:, 0:512], in0=ps, in1=gb[gi], op=ALU.mult),
                         reads=[("ps", bi), ("gb", gi)], writes=["tmp"])
                    p.op("dve", lambda e: e.tensor_tensor(out=hs, in0=hs, in1=tmp[:, 0:512], op=ALU.add),
                         reads=["tmp", ("hres", j)], writes=[("hres", j)])
            if not last:
                for j in range(2):
                    p.dma("sp", hbuf[(t0 + j) * 128:(t0 + j + 1) * 128, :], hres[j], reads=[("hres", j)], writes=[("hbuf", t0 + j)])
            else:
                for j in range(2):
                    p.op("act", lambda e: e.activation(out=ubf, in_=hres[j], func=AF.Square, accum_out=rs2[:, j:j + 1]),
                         reads=[("hres", j)], writes=["ubf", "rs2"])
                p.op("dve", lambda e: e.tensor_scalar(out=rs2, in0=rs2, scalar1=1.0 / D, scalar2=EPS, op0=ALU.mult, op1=ALU.add),
                     reads=["rs2"], writes=["rs2"])
                p.op("act", lambda e: e.activation(out=rs2, in_=rs2, func=AF.Sqrt), reads=["rs2"], writes=["rs2"])
                p.op("dve", lambda e: e.reciprocal(out=rs2, in_=rs2), reads=["rs2"], writes=["rs2"])
                for n in range(8):
                    gi = grr[0] % 2
                    grr[0] += 1
                    p.dma("sp", gb[gi], final_w[n * 512:(n + 1) * 512].partition_broadcast(128), writes=[("gb", gi)])
                    for j in range(2):
                        hs = hres[j][:, n * 512:(n + 1) * 512]
                        p.op("dve", lambda e: e.scalar_tensor_tensor(out=hs, in0=hs, scalar=rs2[:, j:j + 1], in1=gb[gi], op0=ALU.mult, op1=ALU.mult),
                             reads=[("hres", j), "rs2", ("gb", gi)], writes=[("hres", j)])
                for j in range(2):
                    tt = t0 + j - 2
                    p.dma("sp", out[tt * 128:(tt + 1) * 128, :], hres[j], reads=[("hres", j)], writes=[("out", tt)], is_output=True)
    p.barrier()


def stage_NA(p, PD, y, tz_l, winmask, ident_d, need_ctx, heads=None, rows=None):
    nc = p.nc
    scale = NA_DH ** -0.5
    if heads is None:
        heads = range(NA_H)
    if rows is None:
        rows = range(128)
    with ExitStack() as es:
        def sb(name, shape, dt):
            return _ap(es.enter_context(nc.sbuf_tensor(p.uname(name), shape, dt)))
        identb = sb("identb", [128, 128], BF16)
        qT = sb("qT", [128, TOK], BF16)
        kT = sb("kT", [128, TOK], BF16)
        Va = sb("Va", [128, 66, 128], BF16)
        Vb = sb("Vb", [128, 65, 128], BF16)
        Tzm = sb("Tzm", [64, 15, 64], F32)
        wm = sb("wmsb", [64, 64], F32)
        Lb = [sb(f"L{i}", [64, 768], F32) for i in range(2)]
        Eb = [sb(f"E{i}", [64, 768], BF16) for i in range(2)]
        ETb = [sb(f"ET{i}", [128, 384], BF16) for i in range(2)]
        ob = [sb(f"o{i}", [64, 128], F32) for i in range(2)]
        st = [sb(f"st{i}", [64, 4], F32) for i in range(2)]
        p.dma("pool", identb, ident_d, writes=["identb"])
        p.dma("sp", wm, winmask, writes=["wm"])
        it = [0]

        def attn(h, q0, segs, bias, out_tok0):
            i = it[0] % 2
            it[0] += 1
            L, E, ET, o, s = Lb[i], Eb[i], ETb[i], ob[i], st[i]
            nk = sum(sg[1] for sg in segs)
            c = 0
            for si, (k0, n, vch) in enumerate(segs):
                bi, ps = p.psum()
                p.op("pe", lambda e: e.matmul(ps[0:64, 0:n], lhsT=qT[:, q0:q0 + 64], rhs=kT[:, k0:k0 + n], start=True, stop=True),
                     reads=["qT", "kT"], writes=[("ps", bi)])
                if bias is not None and si == 0:
                    p.op("dve", lambda e: e.scalar_tensor_tensor(out=L[:, c:c + n].rearrange("q (i k) -> q i k", k=64),
                                                                 in0=ps[0:64, 0:n].rearrange("q (i k) -> q i k", k=64), scalar=scale,
                                                                 in1=bias, op0=ALU.mult, op1=ALU.add),
                         reads=[("ps", bi), "Tzm"], writes=[("L", i)])
                else:
                    p.op("act", lambda e: e.activation(out=L[:, c:c + n], in_=ps[0:64, 0:n], func=AF.Copy, scale=scale),
                         reads=[("ps", bi)], writes=[("L", i)])
                c += n
            p.op("dve", lambda e: e.tensor_reduce(out=s[:, 0:1], in_=L[:, 0:nk], axis=AX.X, op=ALU.max, negate=True),
                 reads=[("L", i)], writes=[("st", i)])
            p.op("act", lambda e: e.activation(out=E[:, 0:nk], in_=L[:, 0:nk], func=AF.Exp, bias=s[:, 0:1], scale=1.0, accum_out=s[:, 1:2]),
                 reads=[("L", i), ("st", i)], writes=[("E", i), ("st", i)])
            nch = nk // 128
            bi, ps = p.psum()
            psb = ps.bitcast(BF16)
            for j in range(nch):
                p.op("pe", lambda e: e.transpose(out=psb[:, j * 64:(j + 1) * 64], in_=E[:, j * 128:(j + 1) * 128], identity=identb[0:64, 0:64]),
                     reads=[("E", i), "identb"], writes=[("ps", bi)])
            p.op("dve", lambda e: e.tensor_copy(out=ET[:, 0:nch * 64], in_=psb[:, 0:nch * 64]), reads=[("ps", bi)], writes=[("ET", i)])
            vlist = [v for sg in segs for v in sg[2]]
            bi2, ps2 = p.psum()
            for j in range(nch):
                p.op("pe", lambda e: e.matmul(ps2[0:64, 0:128], lhsT=ET[:, j * 64:(j + 1) * 64], rhs=vlist[j], start=(j == 0), stop=(j == nch - 1)),
                     reads=[("ET", i), "V"], writes=[("ps", bi2)])
            p.op("dve", lambda e: e.reciprocal(out=s[:, 2:3], in_=s[:, 1:2]), reads=[("st", i)], writes=[("st", i)])
            p.op("dve", lambda e: e.tensor_scalar(out=o, in0=ps2[0:64, 0:128], scalar1=s[:, 2:3], scalar2=None, op0=ALU.mult),
                 reads=[("ps", bi2), ("st", i)], writes=[("o", i)])
            p.dma("sp", y[out_tok0:out_tok0 + 64, ML_V + h * 128:ML_V + (h + 1) * 128], o, reads=[("o", i)],
                  writes=[("y", out_tok0 // 128)])

        for h in heads:
            allP = [("P", t) for t in range(NTILE)]
            p.dma("pool", qT, PD["naqk"][h * 128:(h + 1) * 128, :], reads=allP, writes=["qT"])
            p.dma("pool", kT, PD["naqk"][NA_W + h * 128:NA_W + (h + 1) * 128, :], reads=allP, writes=["kT"])
            vsrc = PD["nav"][:, h * 128:(h + 1) * 128]
            p.dma("pool", Va, vsrc.rearrange("(j p) d -> p j d", p=128), reads=[("P", t) for t in range(NTILE)], writes=["V"])
            p.dma("pool", Vb, vsrc[64:64 + 65 * 128, :].rearrange("(j p) d -> p j d", p=128), reads=[("P", t) for t in range(NTILE)], writes=["V"])
            p.dma("sp", Tzm, tz_l[h], writes=["Tzm"])
            p.op("dve", lambda e: e.tensor_tensor(out=Tzm, in0=Tzm, in1=wm.unsqueeze(1).broadcast_to([64, 15, 64]), op=ALU.add),
                 reads=["Tzm", "wm"], writes=["Tzm"])
            ctxseg = (0, 256, [Va[:, 0, :], Va[:, 1, :]])
            if need_ctx:
                for cb in range(4):
                    attn(h, cb * 64, [ctxseg], None, cb * 64)
            for r in rows:
                rs = min(max(r - 4, 0), 120)
                off = rs - r + 7
                if rs % 2 == 0:
                    vch = [Va[:, 2 + rs // 2 + j, :] for j in range(4)]
                else:
                    vch = [Vb[:, (3 + rs) // 2 + j, :] for j in range(4)]
                attn(h, 256 + r * 64, [(256 + rs * 64, 512, vch), ctxseg], Tzm[:, off:off + 8, :], 256 + r * 64)
    p.barrier()


def host_na_tables(na_rpb):
    cols = np.arange(GRID_W)
    cidx = np.clip(cols[None, :] - cols[:, None], -15, 15) + 15
    tz = na_rpb[:, :, :, cidx]
    tz = np.ascontiguousarray(np.transpose(tz, (0, 1, 3, 2, 4)))
    cs = np.clip(cols - 8, 0, GRID_W - 16)
    inwin = (cols[None, :] >= cs[:, None]) & (cols[None, :] < cs[:, None] + 16)
    wm = np.where(inwin, 0.0, -30000.0).astype(np.float32)
    return tz.astype(np.float32), wm


def host_ml_tables():
    half = 32
    inv = (np.float32(10000.0) ** (-np.arange(half, dtype=np.float32) / np.float32(half))).astype(np.float32)
    t = np.arange(T)
    row, col = (t // GRID_W).astype(np.float32), (t % GRID_W).astype(np.float32)
    ang_r = row[:, None] * inv[None, :]
    ang_c = col[:, None] * inv[None, :]
    cosE = np.concatenate([np.cos(ang_r), np.cos(ang_r), np.cos(ang_c), np.cos(ang_c)], 1).astype(np.float32)
    sinS = np.concatenate([-np.sin(ang_r), np.sin(ang_r), -np.sin(ang_c), np.sin(ang_c)], 1).astype(np.float32)
    idx = np.arange(64)
    triF = (idx[:, None] <= idx[None, :]).astype(np.float32)
    triB = (idx[:, None] >= idx[None, :]).astype(np.float32)
    maskF = np.where(idx[None, :] <= idx[:, None], 0.0, -1e30).astype(np.float32)
    maskB = np.where(idx[None, :] >= idx[:, None], 0.0, -1e30).astype(np.float32)
    tri = np.stack([triF, triB, maskF, maskB]).astype(np.float32)
    return cosE, sinS, tri


def stage_ML(p, l, PD, y, ml_if_bias, ml_norm_w, ropeC, ropeS, tri_d, ident_d, scr, need_ctx, heads=None, nsteps=None):
    nc = p.nc
    Pml = PD["ml"]
    QK, G, Hml = scr["QK"], scr["G"], scr["Hml"]
    if heads is None:
        heads = range(ML_H)
    NCH = TOK // 64
    with ExitStack() as es:
        def sb(name, shape, dt):
            return _ap(es.enter_context(nc.sbuf_tensor(p.uname(name), shape, dt)))
        bias = sb("mlbias", [128, 24], F32)
        gt = sb("gt", [128, 24], F32)
        gt2 = sb("gt2", [128, 24], F32)
        qk = sb("qk", [128, 1536], F32)
        t1 = sb("t1", [128, 1536], F32)
        t2 = sb("t2", [128, 1536], F32)
        cs = sb("cs", [128, 128], F32)
        sn = sb("sn", [128, 128], F32)
        p.dma("sp", bias, ml_if_bias[l].partition_broadcast(128), writes=["mlbias"])
        for t in range(NTILE):
            rows = slice(t * 128, (t + 1) * 128)
            p.dma("sp", gt, Pml[rows, 4608:4632], reads=[("P", t)], writes=["gt"])
            p.op("dve", lambda e: e.tensor_tensor(out=gt, in0=gt, in1=bias, op=ALU.add), reads=["gt", "mlbias"], writes=["gt"])
            p.op("act", lambda e: e.activation(out=gt[:, 12:24], in_=gt[:, 12:24], func=AF.Exp, scale=-1.0), reads=["gt"], writes=["gt"])
            p.op("act", lambda e: e.activation(out=gt[:, 12:24], in_=gt[:, 12:24], func=AF.Ln, bias=1.0, scale=1.0), reads=["gt"], writes=["gt"])
            p.op("dve", lambda e: e.tensor_scalar(out=gt[:, 12:24], in0=gt[:, 12:24], scalar1=-1.0, scalar2=None, op0=ALU.mult),
                 reads=["gt"], writes=["gt"])
            p.op("dve", lambda e: e.tensor_copy(out=gt2.rearrange("p (x w) -> p x w", w=2), in_=gt.rearrange("p (w x) -> p x w", w=2)),
                 reads=["gt"], writes=["gt2"])
            p.dma("sp", G[rows, :], gt2, reads=["gt2"], writes=[("G", t)])
            p.dma("sp", qk, Pml[rows, 0:1536], reads=[("P", t)], writes=["qk"])
            p.op("act", lambda e: e.activation(out=qk[:, 768:1536], in_=qk[:, 768:1536], func=AF.Copy, scale=ML_DQK ** -0.5),
                 reads=["qk"], writes=["qk"])
            if t >= 2:
                lt = t - 2
                p.dma("sp", cs, ropeC[lt * 128:(lt + 1) * 128, :], writes=["cs"])
                p.dma("sp", sn, ropeS[lt * 128:(lt + 1) * 128, :], writes=["sn"])
                x3 = qk.rearrange("p (h d) -> p h d", d=128)
                p.op("dve", lambda e: e.tensor_tensor(out=t1.rearrange("p (h d) -> p h d", d=128), in0=x3,
                                                      in1=cs.unsqueeze(1).broadcast_to([128, 12, 128]), op=ALU.mult),
                     reads=["qk", "cs"], writes=["t1"])
                x5 = qk.rearrange("p (h a f j) -> p h a f j", a=2, f=2, j=32)
                o5 = t2.rearrange("p (h a f j) -> p h a f j", a=2, f=2, j=32)
                s4 = sn.rearrange("p (a f j) -> p a f j", a=2, f=2)
                for f in range(2):
                    p.op("dve", lambda e: e.tensor_tensor(out=o5[:, :, :, f, :], in0=x5[:, :, :, 1 - f, :],
                                                          in1=s4[:, :, f, :].unsqueeze(1).broadcast_to([128, 12, 2, 32]), op=ALU.mult),
                         reads=["qk", "sn"], writes=["t2"])
                p.op("dve", lambda e: e.tensor_tensor(out=t1, in0=t1, in1=t2, op=ALU.add), reads=["t1", "t2"], writes=["t1"])
                p.dma("sp", QK[rows, :], t1, reads=["t1"], writes=[("QK", t)])
            else:
                p.dma("sp", QK[rows, :], qk, reads=["qk"], writes=[("QK", t)])
    p.barrier()
    with ExitStack() as es:
        def sb(name, shape, dt):
            return _ap(es.enter_context(nc.sbuf_tensor(p.uname(name), shape, dt)))
        identf = sb("identf", [128, 128], F32)
        tri = sb("tri", [64, 4, 64], F32)
        ones = sb("ones", [64, 128], F32)
        p.dma("sp", identf, ident_d, writes=["identf"])
        p.dma("sp", tri, tri_d.rearrange("a p t -> p a t"), writes=["tri"])
        p.op("dve", lambda e: e.memset(ones, 1.0), writes=["ones"])
        B = []
        for d in range(2):
            b = {}
            for nm, shape in [("q", [64, 128]), ("k", [64, 128]), ("va", [64, 257]), ("g", [64, 2]), ("qT", [128, 64]), ("kT", [128, 64]),
                              ("sc64", [64, 12]), ("sc128", [128, 8]), ("m", [128, 1]), ("diagc", [64, 64]), ("dm", [64, 64]), ("W", [64, 64]),
                              ("sw", [64, 64]), ("swT", [64, 64]), ("Asb", [64, 257]), ("tot", [64, 257]), ("ho", [64, 256]),
                              ("kw", [64, 128]), ("C", [128, 257])]:
                b[nm] = sb(f"ml_{nm}{d}", shape, F32)
            B.append(b)

        def key(nm, d):
            return (nm, d)

        def chunk(h, d, c):
            b = B[d]
            K_ = lambda nm: key(nm, d)
            tok = slice(c * 64, (c + 1) * 64)
            tl = c // 2
            p.dma("sp", b["q"], QK[tok, h * 128:(h + 1) * 128], reads=[("QK", tl)], writes=[K_("q")])
            p.dma("sp", b["k"], QK[tok, 768 + h * 128:768 + (h + 1) * 128], reads=[("QK", tl)], writes=[K_("k")])
            p.dma("sp", b["va"][:, 0:256], Pml[tok, 1536 + h * 256:1536 + (h + 1) * 256], reads=[("P", tl)], writes=[K_("va")])
            p.dma("sp", b["g"], G[tok, 2 * (6 * d + h):2 * (6 * d + h) + 2], reads=[("G", tl)], writes=[K_("g")])
            ig, fg = b["g"][:, 0:1], b["g"][:, 1:2]
            s64, s128 = b["sc64"], b["sc128"]
            bsb, cc, rowmax, mi, mq, negmq, winter, enegm, wk, absd, dd, rinv = [s64[:, i:i + 1] for i in range(12)]
            bl, cmax, blm, gmax, mnew, cd, blmn, tmp1 = [s128[:, i:i + 1] for i in range(8)]
            m = b["m"]
            bi, ps = p.psum()
            p.op("pe", lambda e: e.transpose(out=ps[:, 0:64], in_=b["q"], identity=identf[0:64, 0:64]), reads=[K_("q"), "identf"], writes=[("ps", bi)])
            p.op("pe", lambda e: e.transpose(out=ps[:, 64:128], in_=b["k"], identity=identf[0:64, 0:64]), reads=[K_("k"), "identf"], writes=[("ps", bi)])
            p.op("act", lambda e: e.activation(out=b["qT"], in_=ps[:, 0:64], func=AF.Copy), reads=[("ps", bi)], writes=[K_("qT")])
            p.op("act", lambda e: e.activation(out=b["kT"], in_=ps[:, 64:128], func=AF.Copy), reads=[("ps", bi)], writes=[K_("kT")])
            bi, ps = p.psum()
            p.op("pe", lambda e: e.matmul(ps[0:128, 0:1], lhsT=ones, rhs=fg, start=True, stop=True), reads=["ones", K_("g")], writes=[("ps", bi)])
            p.op("pe", lambda e: e.matmul(ps[0:64, 1:2], lhsT=tri[:, d, :], rhs=fg, start=True, stop=True), reads=["tri", K_("g")], writes=[("ps", bi)])
            p.op("dve", lambda e: e.tensor_copy(out=bl, in_=ps[0:128, 0:1]), reads=[("ps", bi)], writes=[K_("s128")])
            p.op("dve", lambda e: e.tensor_copy(out=bsb, in_=ps[0:64, 1:2]), reads=[("ps", bi)], writes=[K_("s64")])
            p.op("dve", lambda e: e.tensor_tensor(out=cc, in0=ig, in1=bsb, op=ALU.subtract), reads=[K_("g"), K_("s64")], writes=[K_("s64")])
            p.op("dve", lambda e: e.tensor_scalar(out=b["diagc"], in0=identf[0:64, 0:64], scalar1=cc, scalar2=None, op0=ALU.mult),
                 reads=["identf", K_("s64")], writes=[K_("diagc")])
            bi, ps = p.psum()
            p.op("pe", lambda e: e.matmul(ps[0:128, 0:64], lhsT=ones, rhs=b["diagc"], start=True, stop=True), reads=["ones", K_("diagc")], writes=[("ps", bi)])
            p.op("dve", lambda e: e.scalar_tensor_tensor(out=b["dm"], in0=ps[0:64, 0:64], scalar=bsb, in1=tri[:, 2 + d, :], op0=ALU.add, op1=ALU.add),
                 reads=[("ps", bi), K_("s64"), "tri"], writes=[K_("dm")])
            p.op("dve", lambda e: e.tensor_reduce(out=cmax, in_=ps[0:128, 0:64], axis=AX.X, op=ALU.max), reads=[("ps", bi)], writes=[K_("s128")])
            p.op("dve", lambda e: e.tensor_reduce(out=rowmax, in_=b["dm"], axis=AX.X, op=ALU.max), reads=[K_("dm")], writes=[K_("s64")])
            p.op("dve", lambda e: e.tensor_tensor(out=blm, in0=bl, in1=m, op=ALU.add), reads=[K_("s128"), K_("m")], writes=[K_("s128")])
            p.op("dve", lambda e: e.tensor_tensor(out=gmax, in0=cmax, in1=bl, op=ALU.add), reads=[K_("s128")], writes=[K_("s128")])
            p.op("dve", lambda e: e.tensor_tensor(out=mnew, in0=blm, in1=gmax, op=ALU.max), reads=[K_("s128")], writes=[K_("s128")])
            p.op("dve", lambda e: e.tensor_tensor(out=mi, in0=bsb, in1=m[0:64, :], op=ALU.add), reads=[K_("s64"), K_("m")], writes=[K_("s64")])
            p.op("dve", lambda e: e.tensor_tensor(out=mq, in0=mi, in1=rowmax, op=ALU.max), reads=[K_("s64")], writes=[K_("s64")])
            p.op("dve", lambda e: e.tensor_scalar(out=negmq, in0=mq, scalar1=-1.0, scalar2=None, op0=ALU.mult), reads=[K_("s64")], writes=[K_("s64")])
            p.op("dve", lambda e: e.tensor_tensor(out=tmp1, in0=blm, in1=mnew, op=ALU.subtract), reads=[K_("s128")], writes=[K_("s128")])
            p.op("dve", lambda e: e.tensor_tensor(out=blmn, in0=bl, in1=mnew, op=ALU.subtract), reads=[K_("s128")], writes=[K_("s128")])
            p.op("act", lambda e: e.activation(out=winter, in_=mi, func=AF.Exp, bias=negmq, scale=1.0), reads=[K_("s64")], writes=[K_("s64")])
            p.op("act", lambda e: e.activation(out=enegm, in_=mq, func=AF.Exp, scale=-1.0), reads=[K_("s64")], writes=[K_("s64")])
            p.op("act", lambda e: e.activation(out=cd, in_=tmp1, func=AF.Exp), reads=[K_("s128")], writes=[K_("s128")])
            p.op("act", lambda e: e.activation(out=wk, in_=cc, func=AF.Exp, bias=blmn[0:64, :], scale=1.0), reads=[K_("s64"), K_("s128")], writes=[K_("s64")])
            p.op("act", lambda e: e.activation(out=b["W"], in_=b["dm"], func=AF.Exp, bias=negmq, scale=1.0), reads=[K_("dm"), K_("s64")], writes=[K_("W")])
            bi, ps = p.psum()
            p.op("pe", lambda e: e.matmul(ps[0:64, 0:64], lhsT=b["qT"], rhs=b["kT"], start=True, stop=True), reads=[K_("qT"), K_("kT")], writes=[("ps", bi)])
            p.op("dve", lambda e: e.tensor_tensor(out=b["sw"], in0=ps[0:64, 0:64], in1=b["W"], op=ALU.mult), reads=[("ps", bi), K_("W")], writes=[K_("sw")])
            bi, ps = p.psum()
            p.op("pe", lambda e: e.transpose(out=ps[0:64, 0:64], in_=b["sw"], identity=identf[0:64, 0:64]), reads=[K_("sw"), "identf"], writes=[("ps", bi)])
            p.op("act", lambda e: e.activation(out=b["swT"], in_=ps[0:64, 0:64], func=AF.Copy), reads=[("ps", bi)], writes=[K_("swT")])
            biA, psA = p.psum()
            p.op("pe", lambda e: e.matmul(psA[0:64, 0:257], lhsT=b["swT"], rhs=b["va"], start=True, stop=True), reads=[K_("swT"), K_("va")], writes=[("ps", biA)])
            biB, psB = p.psum()
            p.op("pe", lambda e: e.matmul(psB[0:64, 0:257], lhsT=b["qT"], rhs=b["C"], start=True, stop=True), reads=[K_("qT"), K_("C")], writes=[("ps", biB)])
            p.op("act", lambda e: e.activation(out=b["Asb"], in_=psA[0:64, 0:257], func=AF.Copy), reads=[("ps", biA)], writes=[K_("Asb")])
            p.op("dve", lambda e: e.scalar_tensor_tensor(out=b["tot"], in0=psB[0:64, 0:257], scalar=winter, in1=b["Asb"], op0=ALU.mult, op1=ALU.add),
                 reads=[("ps", biB), K_("s64"), K_("Asb")], writes=[K_("tot")])
            p.op("dve", lambda e: e.tensor_scalar(out=absd, in0=b["tot"][:, 256:257], scalar1=-1.0, scalar2=None, op0=ALU.mult),
                 reads=[K_("tot")], writes=[K_("s64")])
            p.op("dve", lambda e: e.tensor_tensor(out=absd, in0=absd, in1=b["tot"][:, 256:257], op=ALU.max),
                 reads=[K_("tot"), K_("s64")], writes=[K_("s64")])
            p.op("dve", lambda e: e.tensor_tensor(out=dd, in0=absd, in1=enegm, op=ALU.max), reads=[K_("s64")], writes=[K_("s64")])
            p.op("dve", lambda e: e.reciprocal(out=rinv, in_=dd), reads=[K_("s64")], writes=[K_("s64")])
            p.op("dve", lambda e: e.tensor_scalar(out=b["ho"], in0=b["tot"][:, 0:256], scalar1=rinv, scalar2=None, op0=ALU.mult),
                 reads=[K_("tot"), K_("s64")], writes=[K_("ho")])
            p.dma("sp", Hml[d, tok, h * 256:(h + 1) * 256], b["ho"], reads=[K_("ho")], writes=[("Hml", tl)])
            p.op("dve", lambda e: e.tensor_scalar(out=b["kw"], in0=b["k"], scalar1=wk, scalar2=None, op0=ALU.mult), reads=[K_("k"), K_("s64")], writes=[K_("kw")])
            bi, ps = p.psum()
            p.op("pe", lambda e: e.matmul(ps[0:128, 0:257], lhsT=b["kw"], rhs=b["va"], start=True, stop=True), reads=[K_("kw"), K_("va")], writes=[("ps", bi)])
            p.op("dve", lambda e: e.scalar_tensor_tensor(out=b["C"], in0=b["C"], scalar=cd, in1=ps[0:128, 0:257], op0=ALU.mult, op1=ALU.add),
                 reads=[K_("C"), K_("s128"), ("ps", bi)], writes=[K_("C")])
            p.op("dve", lambda e: e.tensor_copy(out=m, in_=mnew), reads=[K_("s128")], writes=[K_("m")])

        order_f = list(range(NCH))
        order_b = [3, 2, 1, 0] + list(range(NCH - 1, 3, -1))
        if nsteps is not None:
            order_f, order_b = order_f[:nsteps], order_b[:nsteps]
        for h in heads:
            for d in range(2):
                p.op("dve", lambda e: e.memset(B[d]["C"], 0.0), writes=[key("C", d)])
                p.op("dve", lambda e: e.memset(B[d]["m"], 0.0), writes=[key("m", d)])
                p.op("dve", lambda e: e.memset(B[d]["va"][:, 256:257], 1.0), writes=[key("va", d)])
            for i in range(len(order_f)):
                chunk(h, 0, order_f[i])
                chunk(h, 1, order_b[i])
    p.barrier()
    with ExitStack() as es:
        def sb(name, shape, dt):
            return _ap(es.enter_context(nc.sbuf_tensor(p.uname(name), shape, dt)))
        nw = sb("mlnw", [128, 1536], F32)
        hf = sb("hf", [128, 1536], F32)
        hb = sb("hb", [128, 1536], F32)
        og = sb("og", [128, 1536], F32)
        sq = sb("sq", [128, 1536], F32)
        s6 = sb("s6", [128, 6], F32)
        p.dma("sp", nw, ml_norm_w[l].partition_broadcast(128), writes=["mlnw"])
        for t in range(0 if need_ctx else 2, NTILE):
            rows = slice(t * 128, (t + 1) * 128)
            p.dma("sp", hf, Hml[0, rows, :], reads=[("Hml", t)], writes=["hf"])
            p.dma("sp", hb, Hml[1, rows, :], reads=[("Hml", t)], writes=["hb"])
            p.dma("sp", og, Pml[rows, 3072:4608], reads=[("P", t)], writes=["og"])
            p.op("dve", lambda e: e.tensor_tensor(out=hf, in0=hf, in1=hb, op=ALU.add), reads=["hf", "hb"], writes=["hf"])
            p.op("act", lambda e: e.activation(out=sq, in_=hf, func=AF.Square), reads=["hf"], writes=["sq"])
            p.op("dve", lambda e: e.tensor_reduce(out=s6, in_=sq.rearrange("p (h v) -> p h v", v=256), axis=AX.X, op=ALU.add), reads=["sq"], writes=["s6"])
            p.op("dve", lambda e: e.tensor_scalar(out=s6, in0=s6, scalar1=1.0 / 256, scalar2=EPS, op0=ALU.mult, op1=ALU.add), reads=["s6"], writes=["s6"])
            p.op("act", lambda e: e.activation(out=s6, in_=s6, func=AF.Sqrt), reads=["s6"], writes=["s6"])
            p.op("dve", lambda e: e.reciprocal(out=s6, in_=s6), reads=["s6"], writes=["s6"])
            p.op("act", lambda e: e.activation(out=og, in_=og, func=AF.Sigmoid), reads=["og"], writes=["og"])
            p.op("dve", lambda e: e.tensor_tensor(out=hf.rearrange("p (h v) -> p h v", v=256), in0=hf.rearrange("p (h v) -> p h v", v=256),
                                                  in1=s6.unsqueeze(2).broadcast_to([128, 6, 256]), op=ALU.mult), reads=["hf", "s6"], writes=["hf"])
            p.op("dve", lambda e: e.tensor_tensor(out=hf, in0=hf, in1=nw, op=ALU.mult), reads=["hf", "mlnw"], writes=["hf"])
            p.op("dve", lambda e: e.tensor_tensor(out=hf, in0=hf, in1=og, op=ALU.mult), reads=["hf", "og"], writes=["hf"])
            p.dma("sp", y[rows, 0:1536], hf, reads=["hf"], writes=[("y", t)])
    p.barrier()


def alloc_ml_scratch(p):
    return {"QK": p.dram("ml_QK", [TOK, 1536]), "G": p.dram("ml_G", [TOK, 24]), "Hml": p.dram("ml_H", [2, TOK, 1536])}


def alloc_rw_scratch(p):
    n = {}
    for nm in ["W0", "W1r", "KKA0", "KKA1r", "KM0", "KM1", "KM1r", "KK", "KKr", "R", "Rr", "V", "Vr", "GG", "Y0", "Y1r"]:
        n[nm] = p.dram("rw_" + nm, [TOK, RW_W])
    return n


def stage_RW_pre(p, l, PD, scr, rw_mu, rw_w0, rw_w_up, rw_a0, rw_a_up, rw_g_up, rw_k_k, rw_k_a, ident_d, anti_d, tiles=None):
    nc = p.nc
    Prw = PD["rw"]
    if tiles is None:
        tiles = range(NTILE)
    with ExitStack() as es:
        def sb(name, shape, dt):
            return _ap(es.enter_context(nc.sbuf_tensor(p.uname(name), shape, dt)))
        identf = sb("identf", [128, 128], F32)
        Xm = sb("Xm", [128, RW_COLS], F32)
        Xp = sb("Xp", [128, RW_COLS], F32)
        Xn = sb("Xn", [128, RW_COLS], F32)
        mu = sb("mu", [128, RW_COLS], F32)
        wup = sb("wup", [128, 2, RW_W], F32)
        aup = sb("aup", [128, 2, RW_W], F32)
        gup = sb("gup", [128, 4, RW_W], F32)
        w0r = sb("w0r", [1, 2, RW_W], F32)
        a0r = sb("a0r", [1, 2, RW_W], F32)
        ones1 = sb("ones1", [1, 128], F32)
        kkb = sb("kkb", [128, RW_W], F32)
        kab = sb("kab", [128, RW_W], F32)
        Lt = sb("Lt", [128, 768], F32)
        LT = sb("LT", [128, 6, 128], F32)
        at = sb("at", [128, RW_W], F32)
        kk = sb("kkt", [128, RW_W], F32)
        tmp = sb("rtmp", [128, RW_W], F32)
        ob = [sb(f"rob{i}", [128, RW_W], F32) for i in range(2)]
        s24 = sb("s24", [128, 24], F32)
        antiI = sb("antiI", [128, 128], F32)
        orr = [0]

        def outbuf():
            i = orr[0] % 2
            orr[0] += 1
            return i, ob[i]
        p.dma("sp", antiI, anti_d, writes=["antiI"])

        def rev_write(src, srckeys, name, t):
            t0_ = t * 128
            base = (255 - t0_) if t < 2 else (8703 - t0_)
            i, rv = outbuf()
            for cb in range(3):
                cs = slice(cb * 512, (cb + 1) * 512)
                bi, ps = p.psum()
                p.op("pe", lambda e: e.matmul(ps, lhsT=antiI, rhs=src[:, cs], start=True, stop=True), reads=list(srckeys) + ["antiI"], writes=[("ps", bi)])
                p.op("act", lambda e: e.activation(out=rv[:, cs], in_=ps, func=AF.Copy), reads=[("ps", bi)], writes=[("rob", i)])
            p.dma("sp", scr[name][base - 127:base + 1, :], rv, reads=[("rob", i)], writes=[("rw" + name, t)])

        p.dma("sp", identf, ident_d, writes=["identf"])
        p.dma("sp", mu, rw_mu[l].partition_broadcast(128), writes=["mu"])
        p.dma("sp", wup, rw_w_up[l].rearrange("d r c -> r d c"), writes=["wup"])
        p.dma("sp", aup, rw_a_up[l].rearrange("d r c -> r d c"), writes=["aup"])
        for j in range(4):
            r0, r1 = j * 128, min(480, (j + 1) * 128)
            p.dma("sp", gup[0:r1 - r0, j, :], rw_g_up[l, r0:r1, :], writes=["gup"])
        p.dma("sp", w0r, rw_w0[l:l + 1], writes=["w0r"])
        p.dma("sp", a0r, rw_a0[l:l + 1], writes=["a0r"])
        p.dma("sp", kkb, rw_k_k[l].partition_broadcast(128), writes=["kkb"])
        p.dma("sp", kab, rw_k_a[l].partition_broadcast(128), writes=["kab"])
        p.op("dve", lambda e: e.memset(ones1, 1.0), writes=["ones1"])

        for t in tiles:
            t0 = t * 128
            rows = slice(t0, t0 + 128)
            seq0, seq1 = (0, CTX) if t < 2 else (CTX, TOK)
            p.dma("sp", Xm, Prw[rows, :], reads=[("P", t)], writes=["Xm"])
            if t0 == seq0:
                p.op("dve", lambda e: e.memset(Xp[0:1, :], 0.0), writes=["Xp"])
                p.dma("sp", Xp[1:128, :], Prw[t0:t0 + 127, :], reads=[("P", t)], writes=["Xp"])
            else:
                p.dma("sp", Xp, Prw[t0 - 1:t0 + 127, :], reads=[("P", t), ("P", t - 1)], writes=["Xp"])
            if t0 + 128 == seq1:
                p.op("dve", lambda e: e.memset(Xn, 0.0), writes=["Xn"])
                p.dma("sp", Xn[0:127, :], Prw[t0 + 1:t0 + 128, :], reads=[("P", t)], writes=["Xn"])
            else:
                p.dma("sp", Xn, Prw[t0 + 1:t0 + 129, :], reads=[("P", t), ("P", t + 1)], writes=["Xn"])
            p.op("dve", lambda e: e.tensor_tensor(out=Xp, in0=Xp, in1=Xn, op=ALU.add), reads=["Xp", "Xn"], writes=["Xp"])
            p.op("dve", lambda e: e.scalar_tensor_tensor(out=Xp, in0=Xp, scalar=0.5, in1=Xm, op0=ALU.mult, op1=ALU.subtract),
                 reads=["Xp", "Xm"], writes=["Xp"])
            p.op("dve", lambda e: e.tensor_tensor(out=Xp, in0=Xp, in1=mu, op=ALU.mult), reads=["Xp", "mu"], writes=["Xp"])
            p.op("dve", lambda e: e.tensor_tensor(out=Xm, in0=Xm, in1=Xp, op=ALU.add), reads=["Xm", "Xp"], writes=["Xm"])
            rr, kx, vv = Xm[:, 0:1536], Xm[:, 1536:3072], Xm[:, 3072:4608]
            p.dma("sp", scr["R"][rows, :], rr, reads=["Xm"], writes=[("rwR", t)])
            p.dma("sp", scr["V"][rows, :], vv, reads=["Xm"], writes=[("rwV", t)])
            rev_write(rr, ["Xm"], "Rr", t)
            rev_write(vv, ["Xm"], "Vr", t)
            p.op("act", lambda e: e.activation(out=Lt[:, 0:128], in_=Xm[:, 4608:4736], func=AF.Tanh), reads=["Xm"], writes=["Lt"])
            p.op("act", lambda e: e.activation(out=Lt[:, 128:256], in_=Xm[:, 4736:4864], func=AF.Copy), reads=["Xm"], writes=["Lt"])
            p.op("act", lambda e: e.activation(out=Lt[:, 256:736], in_=Xm[:, 4864:5344], func=AF.Sigmoid), reads=["Xm"], writes=["Lt"])
            for g2 in range(2):
                bi, ps = p.psum()
                for j in range(3 * g2, 3 * g2 + 3):
                    wj = 128 if j < 5 else 96
                    p.op("pe", lambda e: e.transpose(out=ps[0:wj, (j % 3) * 128:(j % 3 + 1) * 128], in_=Lt[:, j * 128:j * 128 + wj], identity=identf),
                         reads=["Lt", "identf"], writes=[("ps", bi)])
                if g2 == 0:
                    p.op("act", lambda e: e.activation(out=LT[:, 0:3, :], in_=ps[:, 0:384].rearrange("p (j t) -> p j t", j=3), func=AF.Copy),
                         reads=[("ps", bi)], writes=["LT"])
                else:
                    p.op("act", lambda e: e.activation(out=LT[:, 3:5, :], in_=ps[:, 0:256].rearrange("p (j t) -> p j t", j=2), func=AF.Copy),
                         reads=[("ps", bi)], writes=["LT"])
                    p.op("act", lambda e: e.activation(out=LT[0:96, 5, :], in_=ps[0:96, 256:384], func=AF.Copy), reads=[("ps", bi)], writes=["LT"])
            p.op("dve", lambda e: e.tensor_tensor(out=kk, in0=kx, in1=kkb, op=ALU.mult), reads=["Xm", "kkb"], writes=["kk"])
            p.op("act", lambda e: e.activation(out=tmp, in_=kk, func=AF.Square), reads=["kk"], writes=["rtmp"])
            p.op("dve", lambda e: e.tensor_reduce(out=s24, in_=tmp.rearrange("p (h k) -> p h k", k=64), axis=AX.X, op=ALU.add), reads=["rtmp"], writes=["s24"])
            p.op("act", lambda e: e.activation(out=s24, in_=s24, func=AF.Sqrt), reads=["s24"], writes=["s24"])
            p.op("dve", lambda e: e.tensor_scalar(out=s24, in0=s24, scalar1=1e-12, scalar2=None, op0=ALU.max), reads=["s24"], writes=["s24"])
            p.op("dve", lambda e: e.reciprocal(out=s24, in_=s24), reads=["s24"], writes=["s24"])
            p.op("dve", lambda e: e.tensor_tensor(out=kk.rearrange("p (h k) -> p h k", k=64), in0=kk.rearrange("p (h k) -> p h k", k=64),
                                                  in1=s24.unsqueeze(2).broadcast_to([128, 24, 64]), op=ALU.mult), reads=["kk", "s24"], writes=["kk"])
            p.dma("sp", scr["KK"][rows, :], kk, reads=["kk"], writes=[("rwKK", t)])
            rev_write(kk, ["kk"], "KKr", t)
            for d in range(2):
                oi, o = outbuf()
                for cb in range(3):
                    cs = slice(cb * 512, (cb + 1) * 512)
                    bi, ps = p.psum()
                    p.op("pe", lambda e: e.matmul(ps, lhsT=LT[:, 0, :], rhs=wup[:, d, cs], start=True, stop=False), reads=["LT", "wup"], writes=[("ps", bi)])
                    p.op("pe", lambda e: e.matmul(ps, lhsT=ones1, rhs=w0r[0:1, d, cs], start=False, stop=True), reads=["ones1", "w0r"], writes=[("ps", bi)])
                    p.op("act", lambda e: e.activation(out=tmp[:, cs], in_=ps, func=AF.Sigmoid), reads=[("ps", bi)], writes=["rtmp"])
                p.op("act", lambda e: e.activation(out=o, in_=tmp, func=AF.Exp, scale=-float(np.exp(-0.5))), reads=["rtmp"], writes=[("rob", oi)])
                if d == 0:
                    p.dma("sp", scr["W0"][rows, :], o, reads=[("rob", oi)], writes=[("rwW", d, t)])
                else:
                    rev_write(o, [("rob", oi)], "W1r", t)
                for cb in range(3):
                    cs = slice(cb * 512, (cb + 1) * 512)
                    bi, ps = p.psum()
                    p.op("pe", lambda e: e.matmul(ps, lhsT=LT[:, 1, :], rhs=aup[:, d, cs], start=True, stop=False), reads=["LT", "aup"], writes=[("ps", bi)])
                    p.op("pe", lambda e: e.matmul(ps, lhsT=ones1, rhs=a0r[0:1, d, cs], start=False, stop=True), reads=["ones1", "a0r"], writes=[("ps", bi)])
                    p.op("act", lambda e: e.activation(out=at[:, cs], in_=ps, func=AF.Sigmoid), reads=[("ps", bi)], writes=["at"])
                oi, o = outbuf()
                p.op("dve", lambda e: e.tensor_tensor(out=o, in0=kk, in1=at, op=ALU.mult), reads=["kk", "at"], writes=[("rob", oi)])
                if d == 0:
                    p.dma("sp", scr["KKA0"][rows, :], o, reads=[("rob", oi)], writes=[("rwKKA", d, t)])
                else:
                    rev_write(o, [("rob", oi)], "KKA1r", t)
                oi, o = outbuf()
                p.op("dve", lambda e: e.scalar_tensor_tensor(out=tmp, in0=at, scalar=-1.0, in1=kab, op0=ALU.add, op1=ALU.mult),
                     reads=["at", "kab"], writes=["rtmp"])
                p.op("dve", lambda e: e.scalar_tensor_tensor(out=o, in0=tmp, scalar=1.0, in1=kx, op0=ALU.add, op1=ALU.mult),
                     reads=["rtmp", "Xm"], writes=[("rob", oi)])
                p.dma("sp", scr[f"KM{d}"][rows, :], o, reads=[("rob", oi)], writes=[("rwKM", d, t)])
                if d == 1:
                    rev_write(o, [("rob", oi)], "KM1r", t)
            oi, o = outbuf()
            for cb in range(3):
                cs = slice(cb * 512, (cb + 1) * 512)
                bi, ps = p.psum()
                for j in range(4):
                    kj = 128 if j < 3 else 96
                    p.op("pe", lambda e: e.matmul(ps, lhsT=LT[0:kj, 2 + j, :], rhs=gup[0:kj, j, cs], start=(j == 0), stop=(j == 3)),
                         reads=["LT", "gup"], writes=[("ps", bi)])
                p.op("act", lambda e: e.activation(out=o[:, cs], in_=ps, func=AF.Copy), reads=[("ps", bi)], writes=[("rob", oi)])
            p.dma("sp", scr["GG"][rows, :], o, reads=[("rob", oi)], writes=[("rwGG", t)])
    p.barrier()


def rows_ap(arr, row0, nrows, step, bcast=None):
    C = arr.shape[1]
    dims = [[step * C, nrows], [1, C]]
    if bcast is not None:
        dims = [[0, bcast]] + dims
    return bass.AP(arr.tensor, arr.offset + row0 * C, dims)


def stage_RW_scan(p, scr, ident_d, nchunks=None):
    nc = p.nc
    NCK = TOK // 128
    if nchunks is None:
        nchunks = NCK
    TC = 2
    with ExitStack() as es:
        def sb(name, shape, dt):
            return _ap(es.enter_context(nc.sbuf_tensor(p.uname(name), shape, dt)))
        identf = sb("identf", [128, 128], F32)
        S = sb("S", [128, RW_W], F32)
        t1 = sb("t1", [128, RW_W], F32)
        t2 = sb("t2", [128, RW_W], F32)
        skk = sb("skk", [128, 24], F32)
        names = ["kk", "w", "kka", "km", "r"]
        bc = [{n: sb(f"bc_{n}{i}", [128, TC, RW_W], F32) for n in names} for i in range(2)]
        X = sb("X", [128, 24, 2, 64], F32)
        vT = sb("vT", [128, 24, 128], F32)
        yT = sb("yT", [128, 24, 128], F32)
        Yo = sb("Yo", [128, 24, 2, 64], F32)
        p.dma("sp", identf, ident_d, writes=["identf"])
        p.op("dve", lambda e: e.memset(S, 0.0), writes=["S"])
        S3 = S.rearrange("p (h k) -> p h k", k=64)
        t13 = t1.rearrange("p (h k) -> p h k", k=64)
        t23 = t2.rearrange("p (h k) -> p h k", k=64)
        src = {"kk": ("KK", "KKr"), "w": ("W0", "W1r"), "kka": ("KKA0", "KKA1r"), "km": ("KM0", "KM1r"), "r": ("R", "Rr")}
        allpre = []
        setno = 0
        qrr = 0
        for ck in range(nchunks):
            f0 = ck * 128
            tb0 = (255 - ck * 128) if ck < 2 else (8703 - ck * 128)
            p.dma("sp", X[:, :, 0, :], scr["V"][f0:f0 + 128, :].rearrange("p (h v) -> p h v", v=64), writes=["X"])
            p.dma("sp", X[:, :, 1, :], scr["Vr"][f0:f0 + 128, :].rearrange("p (h v) -> p h v", v=64), writes=["X"])
            for g in range(6):
                bi, ps = p.psum()
                for j in range(4):
                    h = g * 4 + j
                    p.op("pe", lambda e: e.transpose(out=ps[:, j * 128:(j + 1) * 128], in_=X[:, h, :, :].rearrange("p d v -> p (d v)"), identity=identf),
                         reads=["X", "identf"], writes=[("ps", bi)])
                p.op("act", lambda e: e.activation(out=vT[:, g * 4:(g + 1) * 4, :], in_=ps.rearrange("p (j t) -> p j t", j=4), func=AF.Copy),
                     reads=[("ps", bi)], writes=["vT"])
            for s0 in range(0, 128, TC):
                si = setno % 2
                setno += 1
                B = bc[si]
                for n in names:
                    for d in range(2):
                        arr = scr[src[n][d]]
                        sap = rows_ap(arr, f0 + s0, TC, 1, bcast=64)
                        q = "sp" if qrr % 2 == 0 else "act"
                        qrr += 1
                        p.dma(q, B[n][d * 64:(d + 1) * 64, :, :], sap, writes=[("bc", n, si)])
                for s in range(s0, s0 + TC):
                    j = s - s0
                    kk3 = B["kk"][:, j, :].rearrange("p (h k) -> p h k", k=64)
                    kka3 = B["kka"][:, j, :].rearrange("p (h k) -> p h k", k=64)
                    km3 = B["km"][:, j, :].rearrange("p (h k) -> p h k", k=64)
                    p.op("dve", lambda e: e.tensor_tensor(out=t1, in0=S, in1=B["kk"][:, j, :], op=ALU.mult), reads=["S", ("bc", "kk", si)], writes=["t1"])
                    p.op("dve", lambda e: e.tensor_reduce(out=skk, in_=t13, axis=AX.X, op=ALU.add), reads=["t1"], writes=["skk"])
                    p.op("dve", lambda e: e.tensor_tensor(out=t23, in0=kka3, in1=skk.unsqueeze(2).broadcast_to([128, 24, 64]), op=ALU.mult),
                         reads=["skk", ("bc", "kka", si)], writes=["t2"])
                    p.op("dve", lambda e: e.tensor_tensor(out=S, in0=S, in1=B["w"][:, j, :], op=ALU.mult), reads=["S", ("bc", "w", si)], writes=["S"])
                    p.op("dve", lambda e: e.tensor_tensor(out=S, in0=S, in1=t2, op=ALU.subtract), reads=["S", "t2"], writes=["S"])
                    p.op("dve", lambda e: e.tensor_tensor(out=t13, in0=km3, in1=vT[:, :, s:s + 1].broadcast_to([128, 24, 64]), op=ALU.mult),
                         reads=["vT", ("bc", "km", si)], writes=["t1"])
                    p.op("dve", lambda e: e.tensor_tensor(out=S, in0=S, in1=t1, op=ALU.add), reads=["S", "t1"], writes=["S"])
                    p.op("dve", lambda e: e.tensor_tensor(out=t2, in0=S, in1=B["r"][:, j, :], op=ALU.mult), reads=["S", ("bc", "r", si)], writes=["t2"])
                    p.op("dve", lambda e: e.tensor_reduce(out=yT[:, :, s:s + 1], in_=t23, axis=AX.X, op=ALU.add), reads=["t2"], writes=["yT"])
            for g in range(6):
                bi, ps = p.psum()
                for j in range(4):
                    h = g * 4 + j
                    p.op("pe", lambda e: e.transpose(out=ps[:, j * 128:(j + 1) * 128], in_=yT[:, h, :], identity=identf),
                         reads=["yT", "identf"], writes=[("ps", bi)])
                p.op("act", lambda e: e.activation(out=Yo[:, g * 4:(g + 1) * 4, :, :].rearrange("p h d v -> p (h d v)"), in_=ps, func=AF.Copy),
                     reads=[("ps", bi)], writes=["Yo"])
            p.dma("sp", scr["Y0"][f0:f0 + 128, :].rearrange("p (h v) -> p h v", v=64), Yo[:, :, 0, :], reads=["Yo"], writes=[("rwY", 0)])
            p.dma("sp", scr["Y1r"][f0:f0 + 128, :].rearrange("p (h v) -> p h v", v=64), Yo[:, :, 1, :], reads=["Yo"], writes=[("rwY", 1)])
    p.barrier()


def stage_RW_post(p, l, scr, y, rw_r_k, rw_ln_w, rw_ln_b, anti_d, need_ctx):
    nc = p.nc
    with ExitStack() as es:
        def sb(name, shape, dt):
            return _ap(es.enter_context(nc.sbuf_tensor(p.uname(name), shape, dt)))
        antiI = sb("antiI", [128, 128], F32)
        lnw = sb("lnw", [128, RW_W], F32)
        lnb = sb("lnb", [128, RW_W], F32)
        rkb = sb("rkb", [128, RW_W], F32)
        ya = sb("ya", [128, RW_W], F32)
        yb = sb("yb", [128, RW_W], F32)
        k0 = sb("k0", [128, RW_W], F32)
        k1 = sb("k1", [128, RW_W], F32)
        rt = sb("rt", [128, RW_W], F32)
        vt = sb("vt", [128, RW_W], F32)
        gt = sb("gt", [128, RW_W], F32)
        sq = sb("sq", [128, RW_W], F32)
        s24 = sb("s24", [128, 24], F32)
        b24 = sb("b24", [128, 24], F32)
        p.dma("sp", antiI, anti_d, writes=["antiI"])
        p.dma("sp", lnw, rw_ln_w[l].partition_broadcast(128), writes=["lnw"])
        p.dma("sp", lnb, rw_ln_b[l].partition_broadcast(128), writes=["lnb"])
        p.dma("sp", rkb, rw_r_k[l].rearrange("h k -> (h k)").partition_broadcast(128), writes=["rkb"])

        def v3(a):
            return a.rearrange("p (h k) -> p h k", k=64)

        def bc24(a):
            return a.unsqueeze(2).broadcast_to([128, 24, 64])

        for t in range(0 if need_ctx else 2, NTILE):
            t0 = t * 128
            rows = slice(t0, t0 + 128)
            base = (255 - t0) if t < 2 else (8703 - t0)
            p.dma("sp", ya, scr["Y0"][rows, :], writes=["ya"])
            p.dma("sp", yb, scr["Y1r"][base - 127:base + 1, :], writes=["yb"])
            p.dma("act", k0, scr["KM0"][rows, :], writes=["k0"])
            p.dma("act", k1, scr["KM1"][rows, :], writes=["k1"])
            p.dma("sp", rt, scr["R"][rows, :], writes=["rt"])
            p.dma("act", vt, scr["V"][rows, :], writes=["vt"])
            p.dma("sp", gt, scr["GG"][rows, :], writes=["gt"])
            for cb in range(3):
                cs = slice(cb * 512, (cb + 1) * 512)
                bi, ps = p.psum()
                p.op("pe", lambda e: e.matmul(ps, lhsT=antiI, rhs=yb[:, cs], start=True, stop=True), reads=["yb", "antiI"], writes=[("ps", bi)])
                p.op("dve", lambda e: e.tensor_tensor(out=ya[:, cs], in0=ya[:, cs], in1=ps, op=ALU.add), reads=["ya", ("ps", bi)], writes=["ya"])
            p.op("dve", lambda e: e.tensor_reduce(out=s24, in_=v3(ya), axis=AX.X, op=ALU.add), reads=["ya"], writes=["s24"])
            p.op("dve", lambda e: e.tensor_scalar(out=s24, in0=s24, scalar1=1.0 / 64, scalar2=None, op0=ALU.mult), reads=["s24"], writes=["s24"])
            p.op("dve", lambda e: e.tensor_tensor(out=v3(ya), in0=v3(ya), in1=bc24(s24), op=ALU.subtract), reads=["ya", "s24"], writes=["ya"])
            p.op("act", lambda e: e.activation(out=sq, in_=ya, func=AF.Square), reads=["ya"], writes=["sq"])
            p.op("dve", lambda e: e.tensor_reduce(out=s24, in_=v3(sq), axis=AX.X, op=ALU.add), reads=["sq"], writes=["s24"])
            p.op("dve", lambda e: e.tensor_scalar(out=s24, in0=s24, scalar1=1.0 / 64, scalar2=64e-5, op0=ALU.mult, op1=ALU.add), reads=["s24"], writes=["s24"])
            p.op("act", lambda e: e.activation(out=s24, in_=s24, func=AF.Sqrt), reads=["s24"], writes=["s24"])
            p.op("dve", lambda e: e.reciprocal(out=s24, in_=s24), reads=["s24"], writes=["s24"])
            p.op("dve", lambda e: e.tensor_tensor(out=v3(ya), in0=v3(ya), in1=bc24(s24), op=ALU.mult), reads=["ya", "s24"], writes=["ya"])
            p.op("dve", lambda e: e.tensor_tensor(out=ya, in0=ya, in1=lnw, op=ALU.mult), reads=["ya", "lnw"], writes=["ya"])
            p.op("dve", lambda e: e.tensor_tensor(out=ya, in0=ya, in1=lnb, op=ALU.add), reads=["ya", "lnb"], writes=["ya"])
            p.op("dve", lambda e: e.tensor_tensor(out=k0, in0=k0, in1=k1, op=ALU.add), reads=["k0", "k1"], writes=["k0"])
            p.op("dve", lambda e: e.scalar_tensor_tensor(out=k0, in0=k0, scalar=0.5, in1=rt, op0=ALU.mult, op1=ALU.mult), reads=["k0", "rt"], writes=["k0"])
            p.op("dve", lambda e: e.tensor_tensor(out=k0, in0=k0, in1=rkb, op=ALU.mult), reads=["k0", "rkb"], writes=["k0"])
            p.op("dve", lambda e: e.tensor_reduce(out=b24, in_=v3(k0), axis=AX.X, op=ALU.add), reads=["k0"], writes=["b24"])
            p.op("dve", lambda e: e.tensor_tensor(out=v3(vt), in0=v3(vt), in1=bc24(b24), op=ALU.mult), reads=["vt", "b24"], writes=["vt"])
            p.op("dve", lambda e: e.tensor_tensor(out=ya, in0=ya, in1=vt, op=ALU.add), reads=["ya", "vt"], writes=["ya"])
            p.op("dve", lambda e: e.tensor_tensor(out=ya, in0=ya, in1=gt, op=ALU.mult), reads=["ya", "gt"], writes=["ya"])
            p.dma("sp", y[rows, ML_V + NA_W:D], ya, reads=["ya"], writes=[("y", t)])
    p.barrier()


WEIGHT_SPECS = [
    ("mod_w", [DEPTH, D, 6 * D]), ("mod_b", [DEPTH, 6 * D]), ("norm1_w", [DEPTH, D]), ("norm2_w", [DEPTH, D]),
    ("w_in", [DEPTH, D, IN_COLS]), ("ml_if_bias", [DEPTH, 24]), ("ml_norm_w", [DEPTH, ML_V]),
    ("rw_mu", [DEPTH, RW_COLS]), ("rw_w0", [DEPTH, 2, RW_W]), ("rw_w_up", [DEPTH, 2, 128, RW_W]), ("rw_a0", [DEPTH, 2, RW_W]),
    ("rw_a_up", [DEPTH, 2, 128, RW_W]), ("rw_g_up", [DEPTH, 480, RW_W]), ("rw_k_k", [DEPTH, RW_W]), ("rw_k_a", [DEPTH, RW_W]),
    ("rw_r_k", [DEPTH, RW_H, RW_N]), ("rw_ln_w", [DEPTH, RW_W]), ("rw_ln_b", [DEPTH, RW_W]),
    ("w_out", [DEPTH, D, D]), ("w_mlp_up", [DEPTH, D, HID]), ("w_mlp_down", [DEPTH, HID, D]), ("final_norm_w", [D]),
]
CONST_SPECS = [("tz", [DEPTH, NA_H, 64, 15, 64]), ("winmask", [64, 64]), ("ident", [128, 128]), ("anti", [128, 128]),
               ("ropeC", [T, 128]), ("ropeS", [T, 128]), ("tri", [4, 64, 64])]


def build_program(debug=False, depth=DEPTH):
    p = Prog()
    I = {}
    I["xin"] = p.dram("xin", [T, D], kind="ExternalInput")
    I["ctx"] = p.dram("ctx", [CTX, D], kind="ExternalInput")
    I["cvec"] = p.dram("cvec", [2, D], kind="ExternalInput")
    for n, shp in WEIGHT_SPECS + CONST_SPECS:
        I[n] = p.dram(n, shp, kind="ExternalInput")
    out = p.dram("out", [T, D], kind="ExternalOutput")
    hbuf = p.dram("hbuf", [TOK, D])
    y = p.dram("ybuf", [TOK, D])
    modv = p.dram("modv", [DEPTH, 2, 6 * D])
    PD = alloc_P(p)
    mls = alloc_ml_scratch(p)
    rws = alloc_rw_scratch(p)
    p.dma("sp", hbuf[0:CTX, :], I["ctx"], writes=[("hbuf", i) for i in range(2)])
    for i in range(8):
        p.dma("sp", hbuf[CTX + i * 1024:CTX + (i + 1) * 1024, :], I["xin"][i * 1024:(i + 1) * 1024, :],
              writes=[("hbuf", 2 + 8 * i + j) for j in range(8)])
    WB = alloc_wbf(p)
    stage_mod(p, I["cvec"], I["mod_w"], I["mod_b"], modv, I["ident"])
    stage_precast(p, range(depth), I["w_in"], I["w_out"], I["w_mlp_up"], I["w_mlp_down"], WB)
    dbg = {}
    for l in range(depth):
        last = (l == DEPTH - 1)
        need_ctx = not last
        stage_A(p, l, hbuf, modv, I["norm1_w"], WB, PD, I["ident"])
        stage_NA(p, PD, y, I["tz"][l], I["winmask"], I["ident"], need_ctx)
        stage_ML(p, l, PD, y, I["ml_if_bias"], I["ml_norm_w"], I["ropeC"], I["ropeS"], I["tri"], I["ident"], mls, need_ctx)
        stage_RW_pre(p, l, PD, rws, I["rw_mu"], I["rw_w0"], I["rw_w_up"], I["rw_a0"], I["rw_a_up"], I["rw_g_up"], I["rw_k_k"], I["rw_k_a"],
                     I["ident"], I["anti"])
        stage_RW_scan(p, rws, I["ident"])
        stage_RW_post(p, l, rws, y, I["rw_r_k"], I["rw_ln_w"], I["rw_ln_b"], I["anti"], need_ctx)
        if debug and l == 0:
            dbg["y0"] = p.dram("dbg_y0", [TOK, D], kind="ExternalOutput")
            p.dma("sp", dbg["y0"], y, is_output=True)
        stage_C(p, l, hbuf, y, modv, I["norm2_w"], WB, I["ident"], last,
                final_w=I["final_norm_w"], out=out)
        if debug and l == 0:
            dbg["h0"] = p.dram("dbg_h0", [TOK, D], kind="ExternalOutput")
            p.dma("sp", dbg["h0"], hbuf, is_output=True)
    if depth < DEPTH:
        p.dma("sp", out, hbuf[CTX:, :], is_output=True)
    p.finish()
    return p


def host_inputs(inputs, b):
    m = {"xin": np.ascontiguousarray(inputs["x"][b]), "ctx": np.ascontiguousarray(inputs["ctx"][b]),
         "cvec": np.ascontiguousarray(np.stack([inputs["c"][b], inputs["c_ctx"]]))}
    for n, _ in WEIGHT_SPECS:
        m[n] = np.ascontiguousarray(inputs[n], dtype=np.float32)
    tz, wm = host_na_tables(np.asarray(inputs["na_rpb"], dtype=np.float32))
    cosE, sinS, tri = host_ml_tables()
    m.update({"tz": tz, "winmask": wm, "ident": np.eye(128, dtype=np.float32),
              "anti": np.ascontiguousarray(np.eye(128, dtype=np.float32)[::-1]), "ropeC": cosE, "ropeS": sinS, "tri": tri})
    return m


def kernel(**inputs):
    inputs = {k: np.asarray(v) for k, v in inputs.items()}
    p = build_program()
    B = inputs["x"].shape[0]
    in_maps = [host_inputs(inputs, b) for b in range(B)]
    res = run_bass_kernel_spmd(p.nc, in_maps, core_ids=list(range(B)))
    return np.stack([np.asarray(r["out"], dtype=np.float32) for r in res.results], axis=0)
```
